# Optimizing a Trainium2 kernel written in Bass

```python
import math
import jax, jax.numpy as jnp
from jax import lax
import numpy as np

D_MODEL = 1024
BATCH = 16
SEQ = 4096
DEPTH = 1

CHUNK = 64
Q_BLOCK = 128
FOX_HEADS = 8
FOX_HEAD_DIM = 64
FOX_WIDTH = FOX_HEADS * FOX_HEAD_DIM
RET_HEADS = 4
RET_HEAD_DIM = 128
RET_WIDTH = RET_HEADS * RET_HEAD_DIM
D_MIX = FOX_WIDTH + RET_WIDTH
SPLIT_SIZES = [FOX_WIDTH, FOX_WIDTH, FOX_WIDTH, FOX_WIDTH, FOX_HEADS, RET_WIDTH, RET_WIDTH, RET_WIDTH, RET_WIDTH]
D_IN = sum(SPLIT_SIZES)
ROPE_BASE = 10000.0
NORM_EPS = 1e-6
GN_EPS = 1e-5

kernel_name = "hybrid_fox_retention_block"


def rms_norm(x, g):
    xf = x.astype(jnp.float32)
    y = xf * lax.rsqrt(jnp.mean(xf * xf, axis=-1, keepdims=True) + NORM_EPS)
    return (y * g.astype(jnp.float32)).astype(x.dtype)


def forgetting_attention(q, k, v, f_logit):
    b, s, h, dh = q.shape
    scale = dh ** -0.5
    log_f = jax.nn.log_sigmoid(f_logit.astype(jnp.float32))
    c = jnp.cumsum(log_f, axis=1).transpose(0, 2, 1)
    qh = q.transpose(0, 2, 1, 3)
    kh = k.transpose(0, 2, 1, 3)
    vh = v.transpose(0, 2, 1, 3)
    outs = []
    for blk in range(s // Q_BLOCK):
        q0 = blk * Q_BLOCK
        q1 = q0 + Q_BLOCK
        qb = qh[:, :, q0:q1].astype(jnp.float32)
        kb = kh[:, :, :q1].astype(jnp.float32)
        vb = vh[:, :, :q1]
        logits = jnp.einsum('bhqd,bhkd->bhqk', qb, kb) * scale
        logits = logits + c[:, :, q0:q1, None] - c[:, :, None, :q1]
        qpos = jnp.arange(q0, q1)[:, None]
        kpos = jnp.arange(q1)[None, :]
        logits = jnp.where(kpos <= qpos, logits, -jnp.inf)
        p = jax.nn.softmax(logits, axis=-1)
        outs.append(jnp.einsum('bhqk,bhkd->bhqd', p.astype(vb.dtype), vb))
    o = jnp.concatenate(outs, axis=2)
    return o.transpose(0, 2, 1, 3)


def rotary(x, pos):
    d = x.shape[-1]
    half = d // 2
    inv = 1.0 / (ROPE_BASE ** (jnp.arange(half, dtype=jnp.float32) / half))
    ang = pos.astype(jnp.float32)[:, None] * inv[None, :]
    cos = jnp.cos(ang)[None, :, None, :]
    sin = jnp.sin(ang)[None, :, None, :]
    xf = x.astype(jnp.float32)
    x1, x2 = xf[..., :half], xf[..., half:]
    return jnp.concatenate([x1 * cos - x2 * sin, x1 * sin + x2 * cos], axis=-1)


def retention(q, k, v):
    b, s, h, dk = q.shape
    dv = v.shape[-1]
    nc = s // CHUNK
    log_gamma = jnp.log1p(-jnp.exp(jnp.linspace(math.log(1.0 / 32), math.log(1.0 / 512), h))).astype(jnp.float32)
    pos = jnp.arange(s)
    qf = rotary(q, pos)
    kf = rotary(k, pos) * (dk ** -0.5)
    vf = v.astype(jnp.float32)
    qc = qf.reshape(b, nc, CHUNK, h, dk)
    kc = kf.reshape(b, nc, CHUNK, h, dk)
    vc = vf.reshape(b, nc, CHUNK, h, dv)
    idx = jnp.arange(CHUNK, dtype=jnp.float32)
    dist = jnp.abs(idx[:, None] - idx[None, :])
    decay_in = jnp.exp(log_gamma[:, None, None] * dist)
    scores = jnp.einsum('bnihd,bnjhd->bnhij', qc, kc) * decay_in
    inner = jnp.einsum('bnhij,bnjhv->bnihv', scores, vc)
    q_decay = jnp.exp(log_gamma[None, :] * (idx[:, None] + 1.0))
    k_decay = jnp.exp(log_gamma[None, :] * (CHUNK - 1.0 - idx[:, None]))
    chunk_decay = jnp.exp(log_gamma * CHUNK)

    def step(state, xs):
        qi, ki, vi = xs
        cross = jnp.einsum('bihk,bhkv->bihv', qi, state) * q_decay[None, :, :, None]
        state = state * chunk_decay[None, :, None, None] + jnp.einsum('bjhk,bjhv->bhkv', ki * k_decay[None, :, :, None], vi)
        return state, cross

    state0 = jnp.zeros((b, h, dk, dv), jnp.float32)
    _, cross = lax.scan(step, state0, (qc.transpose(1, 0, 2, 3, 4), kc.transpose(1, 0, 2, 3, 4), vc.transpose(1, 0, 2, 3, 4)))
    cross = cross.transpose(1, 0, 2, 3, 4)
    o = (inner + cross).reshape(b, s, h, dv)
    mu = jnp.mean(o, axis=-1, keepdims=True)
    var = jnp.mean(jnp.square(o - mu), axis=-1, keepdims=True)
    return (o - mu) * lax.rsqrt(var + GN_EPS)


def setup_inputs(seed: int = 0) -> dict:
    key = jax.random.key(seed)
    kx, kg1, kin, kfb, kout, kg2 = jax.random.split(key, 6)
    x = jax.random.normal(kx, (BATCH, SEQ, D_MODEL), jnp.float32)
    pre_norm_gain = 1.0 + 0.02 * jax.random.normal(kg1, (DEPTH, D_MODEL), jnp.float32)
    w_in = jax.random.normal(kin, (DEPTH, D_MODEL, D_IN), jnp.float32) * (D_MODEL ** -0.5)
    fox_forget_bias = jnp.linspace(1.0, 6.0, FOX_HEADS, dtype=jnp.float32)[None, :] + 0.01 * jax.random.normal(kfb, (DEPTH, FOX_HEADS), jnp.float32)
    w_out = jax.random.normal(kout, (DEPTH, D_MIX, D_MODEL), jnp.float32) * (D_MIX ** -0.5)
    post_norm_gain = 1.0 + 0.02 * jax.random.normal(kg2, (DEPTH, D_MODEL), jnp.float32)
    return {"x": x, "pre_norm_gain": pre_norm_gain, "w_in": w_in, "fox_forget_bias": fox_forget_bias, "w_out": w_out, "post_norm_gain": post_norm_gain}


def reference(x, pre_norm_gain, w_in, fox_forget_bias, w_out, post_norm_gain):
    b, s, _ = x.shape
    split_points = [int(v) for v in np.cumsum(SPLIT_SIZES)[:-1]]
    h = x
    for layer in range(DEPTH):
        u = rms_norm(h, pre_norm_gain[layer])
        proj = jnp.einsum('bsd,de->bse', u, w_in[layer])
        fq, fk, fv, fz, ff, rq, rk, rv, rz = jnp.split(proj, split_points, axis=-1)
        f_logit = ff + fox_forget_bias[layer].astype(ff.dtype)
        fox = forgetting_attention(fq.reshape(b, s, FOX_HEADS, FOX_HEAD_DIM), fk.reshape(b, s, FOX_HEADS, FOX_HEAD_DIM), fv.reshape(b, s, FOX_HEADS, FOX_HEAD_DIM), f_logit)
        fox = fox.reshape(b, s, FOX_WIDTH).astype(x.dtype) * jax.nn.silu(fz)
        ret = retention(rq.reshape(b, s, RET_HEADS, RET_HEAD_DIM), rk.reshape(b, s, RET_HEADS, RET_HEAD_DIM), rv.reshape(b, s, RET_HEADS, RET_HEAD_DIM))
        ret = ret.reshape(b, s, RET_WIDTH).astype(x.dtype) * jax.nn.silu(rz)
        mixed = jnp.concatenate([fox, ret], axis=-1)
        out = jnp.einsum('bse,ed->bsd', mixed, w_out[layer])
        h = h + rms_norm(out, post_norm_gain[layer])
    return h
```

```python
import math
import os
from contextlib import ExitStack

import numpy as np
import ml_dtypes
import concourse.bass as bass
import concourse.mybir as mybir
from concourse.bass_utils import run_bass_kernel_spmd

F32 = mybir.dt.float32
BF16 = mybir.dt.bfloat16
ALU = mybir.AluOpType
AF = mybir.ActivationFunctionType

D_MODEL = 1024
D_IN = 4104
FOX_HEADS = 8
RET_HEADS = 4
NORM_EPS = 1e-6
GN_EPS = 1e-5
N_CORES = 8

SAME_ENGINE_SYNC = bool(int(os.environ.get("KSES", "1")))

STAGE = float(os.environ.get('KSTAGE', '99'))
NDS = 16
DUMP = bool(int(os.environ.get('KDUMP', '0')))


class Res:
    __slots__ = ("name", "last_w", "readers")

    def __init__(self, name):
        self.name = name
        self.last_w = None
        self.readers = []


class Op:
    __slots__ = ("fn", "deps", "dma_k", "needed")

    def __init__(self, fn, deps, dma_k=None):
        self.fn = fn
        self.deps = deps
        self.dma_k = dma_k
        self.needed = False


class Sched:
    ENGS = ("SP", "PE", "ACT", "DVE", "POOL")

    def __init__(self):
        self.ops = {e: [] for e in self.ENGS}
        self.n_dma = 0
        self.n_qdma = 0

    def op(self, eng, fn, reads=(), writes=()):
        deps = set()
        for r in reads:
            if r.last_w is not None:
                deps.add(r.last_w)
        for w in writes:
            if w.last_w is not None:
                deps.add(w.last_w)
            for rd in w.readers:
                deps.add(rd)
        idx = len(self.ops[eng])
        me = (eng, idx)
        deps.discard(me)
        self.ops[eng].append(Op(fn, deps))
        for r in reads:
            r.readers.append(me)
        for w in writes:
            w.last_w = me
            w.readers = []
        return me

    def dma(self, fn, reads=(), writes=(), eng="SP"):
        me = self.op(eng, fn, reads, writes)
        if eng == "SP":
            self.ops[eng][me[1]].dma_k = self.n_dma
            self.n_dma += 1
        else:
            self.ops[eng][me[1]].dma_k = -1 - self.n_qdma
            self.n_qdma += 1
        return me

    def emit(self, nc, block, sems, dsems, qsems):
        for e in self.ENGS:
            for o in self.ops[e]:
                for (e2, i2) in o.deps:
                    if self.ops[e2][i2].dma_k is not None:
                        continue
                    if e2 == e and (e == "PE" or not SAME_ENGINE_SYNC):
                        continue
                    self.ops[e2][i2].needed = True
        cnt = {}
        for e in self.ENGS:
            if e == "SP":
                continue
            c = 0
            arr = []
            for o in self.ops[e]:
                if o.needed:
                    c += 1
                arr.append(c)
            cnt[e] = arr

        def run(e, engobj):
            waited = {}
            for idx, o in enumerate(self.ops[e]):
                need = {}
                for (e2, i2) in o.deps:
                    if self.ops[e2][i2].dma_k is not None:
                        k = self.ops[e2][i2].dma_k
                        if k < 0:
                            key = ("Q", -1 - k)
                            val = 16
                        else:
                            key = ("D", k % NDS)
                            val = 16 * (k // NDS + 1)
                    else:
                        if e2 == e and (e == "PE" or not SAME_ENGINE_SYNC):
                            continue
                        key = ("C", e2)
                        val = cnt[e2][i2]
                    if val > need.get(key, 0):
                        need[key] = val
                if o.dma_k is not None and o.dma_k >= NDS:
                    key = ("D", o.dma_k % NDS)
                    val = 16 * (o.dma_k // NDS)
                    if val > need.get(key, 0):
                        need[key] = val
                for key, val in need.items():
                    if waited.get(key, 0) >= val:
                        continue
                    waited[key] = val
                    s = dsems[key[1]] if key[0] == "D" else (qsems[key[1]] if key[0] == "Q" else sems[key[1]])
                    engobj.wait_ge(s, val)
                    if DUMP:
                        print("   ", e, idx, "WAIT", key, val)
                ins = o.fn(engobj)
                if DUMP:
                    print(e, idx, "deps", sorted(o.deps), "inc" if o.needed else "", o.dma_k)
                if o.dma_k is not None and o.dma_k < 0:
                    ins.then_inc(qsems[-1 - o.dma_k], 16)
                elif o.dma_k is not None:
                    ins.then_inc(dsems[o.dma_k % NDS], 16)
                elif o.needed:
                    ins.then_inc(sems[e], 1)
            if e == "SP":
                for s in range(min(NDS, self.n_dma)):
                    last_k = ((self.n_dma - 1 - s) // NDS) * NDS + s
                    engobj.wait_ge(dsems[s], 16 * (last_k // NDS + 1))

        @block.sync
        def _(e):
            run("SP", e)

        @block.tensor
        def _(e):
            run("PE", e)

        @block.scalar
        def _(e):
            run("ACT", e)

        @block.vector
        def _(e):
            run("DVE", e)

        @block.gpsimd
        def _(e):
            run("POOL", e)


def bcast_ap(ap, dims):
    return bass.AP(tensor=ap.tensor, offset=ap.offset, ap=[list(d) for d in dims])


def make_consts(seq_len):
    lg = np.log1p(-np.exp(np.linspace(math.log(1.0 / 32), math.log(1.0 / 512), RET_HEADS))).astype(np.float32)
    lg64 = lg.astype(np.float64)
    ident = np.eye(128, dtype=np.float32).astype(ml_dtypes.bfloat16)
    s_idx = np.arange(128)[:, None]
    t_idx = np.arange(128)[None, :]
    maskneg = np.where(s_idx <= t_idx, 0.0, -30000.0).astype(np.float32).astype(ml_dtypes.bfloat16)
    triu = (s_idx <= t_idx).astype(np.float32)
    j = np.arange(128)[:, None]
    i = np.arange(128)[None, :]
    same = (i // 64) == (j // 64)
    fwd = (j // 64 == 0) & (i // 64 == 1)
    dT = np.zeros((128, RET_HEADS, 128), np.float32)
    for h in range(RET_HEADS):
        d = np.where(same, np.exp(lg64[h] * np.abs(i - j)), np.where(fwd, np.exp(lg64[h] * (i - j)), 0.0))
        dT[:, h, :] = (d * (128.0 ** -0.5)).astype(np.float32)
    il = np.arange(128, dtype=np.float64)
    qdec = np.stack([np.exp(lg64[h] * (il + 1.0)) for h in range(RET_HEADS)], 0).astype(np.float32)
    kdec = np.stack([np.exp(lg64[h] * (127.0 - il)) * (128.0 ** -0.5) for h in range(RET_HEADS)], 1).astype(np.float32)
    g128 = [float(np.exp(lg64[h] * 128.0)) for h in range(RET_HEADS)]
    half = 64
    inv = (1.0 / (10000.0 ** (np.arange(half, dtype=np.float32) / half))).astype(np.float32)
    ang = np.arange(seq_len, dtype=np.float32)[:, None] * inv[None, :]
    cossin = np.concatenate([np.cos(ang), np.sin(ang)], axis=1).astype(np.float32)
    return dict(ident=ident, maskneg=maskneg, triu=triu, dT=dT.reshape(128, RET_HEADS * 128),
                qdec=qdec.reshape(1, RET_HEADS * 128), kdec=kdec, cossin=cossin), g128


def build_nc(NSEQ, NT, g128):
    nc = bass.Bass("TRN2", target_bir_lowering=False, dynamic_dma_scratch_size=4096)
    NTOK = NSEQ * NT * 128
    SEQL = NT * 128
    x_d = nc.dram_tensor("x", [NTOK, D_MODEL], F32, kind="ExternalInput").ap()
    win_d = nc.dram_tensor("w_in", [D_MODEL, D_IN], F32, kind="ExternalInput").ap()
    wout_d = nc.dram_tensor("w_out", [D_MODEL, D_MODEL], F32, kind="ExternalInput").ap()
    g1t_d = nc.dram_tensor("g1t", [128, 8], F32, kind="ExternalInput").ap()
    g2_d = nc.dram_tensor("g2", [1, D_MODEL], F32, kind="ExternalInput").ap()
    fb_d = nc.dram_tensor("fb", [1, FOX_HEADS], F32, kind="ExternalInput").ap()
    ident_d = nc.dram_tensor("ident", [128, 128], BF16, kind="ExternalInput").ap()
    mask_d = nc.dram_tensor("maskneg", [128, 128], BF16, kind="ExternalInput").ap()
    triu_d = nc.dram_tensor("triu", [128, 128], F32, kind="ExternalInput").ap()
    dT_d = nc.dram_tensor("dT", [128, 512], F32, kind="ExternalInput").ap()
    qdec_d = nc.dram_tensor("qdec", [1, 512], F32, kind="ExternalInput").ap()
    kdec_d = nc.dram_tensor("kdec", [128, 4], F32, kind="ExternalInput").ap()
    cs_d = nc.dram_tensor("cossin", [SEQL, 128], F32, kind="ExternalInput").ap()
    y_d = nc.dram_tensor("y", [NTOK, D_MODEL], F32, kind="ExternalOutput").ap()

    S = Sched()
    es = ExitStack()

    def sb(name, shape, dt):
        return es.enter_context(nc.sbuf_tensor(name, shape, dt))

    def ps(name, shape, dt):
        return es.enter_context(nc.psum_tensor(name, shape, dt))

    with es:
        W = sb("W", [128, 8, D_IN], BF16)
        WO = sb("WO", [128, 8, D_MODEL], BF16)
        KT = sb("KT", [128, 4, SEQL], BF16)
        VA = sb("VA", [128, NT, FOX_HEADS, 65], BF16)
        KT_r = [Res(f"KT{t}") for t in range(NT)]
        VA_r = [Res(f"VA{t}") for t in range(NT)]
        a_all = sb("a_all", [128, NT, FOX_HEADS], F32)
        aall_r = Res("a_all")
        g2t = sb("g2t", [128, D_MODEL], F32)
        fbt = sb("fbt", [128, FOX_HEADS], F32)
        g1t = sb("g1t_s", [128, 8], F32)
        NBIG = 2
        big = [sb(f"big{i}", [128, 1024], F32) for i in range(NBIG)]
        big_r = [Res(f"big{i}") for i in range(NBIG)]
        xn = sb("xn", [128, 1024], BF16)
        xn_r = Res("xn")
        junk = sb("junk", [128, 512], BF16)
        _jr = Res("junk")
        junk_r = [_jr, _jr]
        uT = [sb(f"uT{k}", [128, 8, 128], BF16) for k in range(2)]
        uT_r = [[Res(f"uT{k}_{c}") for c in range(8)] for k in range(2)]
        qk = sb("qk", [128, 1024], BF16)
        qk_r = [[Res(f"qk{a}_{b}") for b in range(2)] for a in range(2)]
        QT = [sb(f"QT{k}", [128, 4, 2, 256], BF16) for k in range(2)]
        QT_r = [[[Res(f"QT{k}_{q}_{e}") for e in range(2)] for q in range(2)] for k in range(2)]
        gfox = [sb(f"gfox{k}", [128, 512], F32) for k in range(4)]
        gfox_r = [Res(f"gfox{k}") for k in range(4)]
        gret = sb("gret", [128, 512], F32)
        gret_r = Res("gret")
        th = sb("th", [128, 512], F32)
        thA_r, thB_r = Res("thA"), Res("thB")
        th_r = [thA_r, thB_r]
        rot, rot_r = qk, qk_r
        kd = sb("kd", [128, 512], BF16)
        kd_r = [Res(f"kd{h}") for h in range(4)]
        rqT = sb("rqT", [128, 4, 128], BF16)
        rkT = sb("rkT", [128, 4, 128], BF16)
        rqdT = sb("rqdT", [128, 4, 128], BF16)
        rqT_r, rkT_r, rqdT_r = Res("rqT"), Res("rkT"), Res("rqdT")
        rv = sb("rv", [128, 512], BF16)
        rv_r = Res("rv")
        st_f = sb("st_f", [128, 4, 128], F32)
        st_b = sb("st_b", [128, 4, 128], BF16)
        st_f_r = [Res(f"stf{h}") for h in range(4)]
        st_b_r = [Res(f"stb{h}") for h in range(4)]
        NPT = 6
        PT = sb("PT", [128, NPT, 256], BF16)
        PT_r = [Res(f"PT{i}") for i in range(NPT)]
        Sd = sb("Sd", [128, 2, 128], BF16)
        Sd_r = [Res("Sd0"), Res("Sd1")]
        mixed = [sb(f"mixed{k}", [128, 1024], BF16) for k in range(4)]
        mixf_r = [Res(f"mixf{k}") for k in range(4)]
        mixr_r = [[Res(f"mixr{k}_{h}") for h in range(4)] for k in range(4)]
        mixT = sb("mixT", [128, 8, 128], BF16)
        mixT_r = Res("mixT")
        ident = sb("ident_s", [128, 128], BF16)
        maskneg = sb("mask_s", [128, 128], BF16)
        triu = sb("triu_s", [128, 128], F32)
        onesf = sb("onesf", [128, 128], F32)
        dT = sb("dT_s", [128, 4, 128], F32)
        qdec = sb("qdec_s", [128, 4, 128], F32)
        kdec = sb("kdec_s", [128, 4], F32)
        cst = sb("cst", [128, 2, 128], F32)
        cst_r = [Res("cs0"), Res("cs1")]
        smA = sb("smA", [128, 8], F32)
        smB = sb("smB", [128, 64], F32)
        smC = sb("smC", [128, 2, 8], F32)
        smD = sb("smD", [128, 8], F32)
        negh = sb("negh", [128, 1], F32)
        bn = sb("bn", [128, 4, 8], F32)
        biasm = [sb(f"biasm{k}", [128, NT, FOX_HEADS], F32) for k in range(2)]
        biasm_r = [[Res(f"biasm{k}_{h}") for h in range(FOX_HEADS)] for k in range(2)]
        carry = sb("carry", [128, 2, FOX_HEADS], F32)
        lf = sb("lf", [128, 2 * FOX_HEADS], F32)
        lfe_r = [Res("lfe0"), Res("lfe1")]
        W_r = [[Res(f"W{kc}_{pc}") for pc in range(4)] for kc in range(8)]
        WO_r = [Res(f"WO{kc}") for kc in range(8)]
        cr = {n: Res("c_" + n) for n in ["ident", "mask", "triu", "dT", "qdec", "kdec", "g1t", "g2t", "fbt", "ones", "negh"]}
        ssA_r, msA_r, rstdA_r = Res("ssA"), Res("msA"), Res("rstdA")
        ssD_r, msD_r, rstdD_r = [Res("ssD0"), Res("ssD1")], Res("msD"), Res("rstdD")
        rl_r = [Res("rl0"), Res("rl1")]
        aref_r = Res("aref")
        gn_r = [Res(f"gn{h}") for h in range(4)]
        bn_r = [Res(f"bn{h}") for h in range(4)]
        carry_r = [Res("carry0"), Res("carry1")]
        lf_r = Res("lf")

        pj = [ps(f"pj{i}", [128, 512], F32) for i in range(2)]
        pj_r = [Res(f"pj{i}") for i in range(2)]
        ptr = [ps(f"ptr{i}", [128, 8, 128], BF16) for i in range(2)]
        ptr_r = [Res(f"ptr{i}") for i in range(2)]
        pS = [ps(f"pS{i}", [128, 4, 128], F32) for i in range(2)]
        pS_r = [Res(f"pS{i}") for i in range(2)]
        pO = [ps(f"pO{i}", [128, 512], F32) for i in range(2)]
        pO_r = [Res(f"pO{i}") for i in range(2)]

        cnt = {"pj": 0, "ptr": 0, "pS": 0, "PT": 0, "Sd": 0}

        def nxt(k, n):
            v = cnt[k] % n
            cnt[k] += 1
            return v

        S.dma(lambda e: e.dma_start(out=ident[:, :], in_=ident_d[:, :]), writes=[cr["ident"]])
        S.dma(lambda e: e.dma_start(out=maskneg[:, :], in_=mask_d[:, :]), writes=[cr["mask"]])
        S.dma(lambda e: e.dma_start(out=triu[:, :], in_=triu_d[:, :]), writes=[cr["triu"]])
        S.dma(lambda e: e.dma_start(out=dT[:, :, :].rearrange("p a b -> p (a b)"), in_=dT_d[:, :]), writes=[cr["dT"]])
        S.dma(lambda e: e.dma_start(out=qdec[:, :, :].rearrange("p a b -> p (a b)"),
                                    in_=bcast_ap(qdec_d, [[0, 128], [1, 512]])), writes=[cr["qdec"]])
        S.dma(lambda e: e.dma_start(out=kdec[:, :], in_=kdec_d[:, :]), writes=[cr["kdec"]])
        S.dma(lambda e: e.dma_start(out=g1t[:, :], in_=g1t_d[:, :]), writes=[cr["g1t"]])
        S.dma(lambda e: e.dma_start(out=g2t[:, :], in_=bcast_ap(g2_d, [[0, 128], [1, D_MODEL]])), writes=[cr["g2t"]])
        S.dma(lambda e: e.dma_start(out=fbt[:, :], in_=bcast_ap(fb_d, [[0, 128], [1, FOX_HEADS]])), writes=[cr["fbt"]])
        S.op("POOL", lambda e: e.memset(onesf[:, :], 1.0), writes=[cr["ones"]])
        S.op("POOL", lambda e: e.memset(negh[:, :], -0.5), writes=[cr["negh"]])
        S.op("POOL", lambda e: e.memset(VA[:, :, :, :].rearrange("p a b c -> p (a b c)"), 1.0), writes=VA_r)
        for k in range(2):
            S.op("POOL", lambda e, k=k: e.memset(QT[k][:, :, :, :].rearrange("p a b c -> p (a b c)"), 0.0),
                 writes=[QT_r[k][q][e_] for q in range(2) for e_ in range(2)])

        nb = 0
        for pc in (2, 3, 0, 1):
            for kc in range(8):
                c0 = pc * 1026
                S.dma(lambda e, kc=kc, c0=c0: e.dma_start(out=W[:, kc, c0:c0 + 1026], in_=win_d[kc * 128:(kc + 1) * 128, c0:c0 + 1026],
                                                          max_dma_last_dim=4104),
                      writes=[W_r[kc][pc]], eng="POOL")
        for kc in range(8):
            S.dma(lambda e, kc=kc: e.dma_start(out=WO[:, kc, :], in_=wout_d[kc * 128:(kc + 1) * 128, :], max_dma_last_dim=4096),
                  writes=[WO_r[kc]], eng="POOL")

        fill = {"rate": 0.0, "credit": 0.0, "fn": None}

        def lag(n=2):
            if fill["fn"] is not None and fill["rate"] > 0.0:
                fill["fn"](n)

        def tick(n=1):
            if fill["fn"] is None or fill["rate"] <= 0.0:
                return
            fill["credit"] += n * fill["rate"]
            if fill["credit"] >= 1.0:
                k_ = int(fill["credit"])
                fill["credit"] -= k_
                fill["fn"](k_)

        def transposes(src, src_r, pt):
            for c in range(8):
                S.op("PE", lambda e, c=c: e.transpose(out=ptr[pt][:, c, :], in_=src[:, c * 128:(c + 1) * 128], identity=ident[:, :]),
                     reads=(src_r[c // 4] if isinstance(src_r, list) else [src_r]) + [cr["ident"]], writes=[ptr_r[pt]])
                tick()

        def stage_A(g):
            i = g % NT
            X, X_r = big[0], big_r[0]
            S.dma(lambda e: e.dma_start(out=X[:, :], in_=x_d[g * 128:(g + 1) * 128, :]), writes=[X_r])
            S.dma(lambda e: e.dma_start(out=cst[:, g % 2, :], in_=cs_d[i * 128:(i + 1) * 128, :]), writes=[cst_r[g % 2]])
            yield
            S.op("ACT", lambda e: e.activation(out=xn[:, :], in_=X[:, :], func=AF.Square, accum_out=smA[:, 0:1]),
                 reads=[X_r], writes=[xn_r, ssA_r])
            yield
            S.op("DVE", lambda e: e.tensor_scalar(out=smA[:, 1:2], in0=smA[:, 0:1], scalar1=1.0 / D_MODEL,
                                                  scalar2=NORM_EPS, op0=ALU.mult, op1=ALU.add), reads=[ssA_r], writes=[msA_r])
            yield
            S.op("POOL", lambda e: e.tensor_tensor(out=smA[:, 2:3], in0=smA[:, 1:2], in1=negh[:, :], op=ALU.pow),
                 reads=[msA_r, cr["negh"]], writes=[rstdA_r])
            yield
            S.op("DVE", lambda e: e.tensor_scalar(out=xn[:, :], in0=X[:, :], scalar1=smA[:, 2:3], scalar2=None, op0=ALU.mult),
                 reads=[X_r, rstdA_r], writes=[xn_r])
            yield
            pt = nxt("ptr", 2)
            transposes(xn, xn_r, pt)
            for c in range(8):
                S.op("DVE", lambda e, c=c: e.tensor_scalar(out=uT[g % 2][:, c, :], in0=ptr[pt][:, c, :], scalar1=g1t[:, c:c + 1], scalar2=None,
                                                           op0=ALU.mult), reads=[ptr_r[pt], cr["g1t"]], writes=[uT_r[g % 2][c]])

        def stage_B(g):
            i = g % NT
            pb = g % 2
            pm = (g // 2) % 2
            cs = g % 2
            U, U_r = uT[pb], uT_r[pb]

            def proj(c0, ncols):
                p = nxt("pj", 2)
                for kc in range(8):
                    S.op("PE", lambda e, kc=kc: e.matmul(pj[p][:, 0:ncols], lhsT=U[:, kc, :], rhs=W[:, kc, c0:c0 + ncols],
                                                         start=(kc == 0), stop=(kc == 7)),
                         reads=[U_r[kc]] + W_r[kc][c0 // 1026:(c0 + ncols - 1) // 1026 + 1], writes=[pj_r[p]])
                    tick()
                return p

            evac = "ACT" if i < 22 else "DVE"

            def copy_from_psum(out_ap, in_ap, reads, writes):
                if evac == "ACT":
                    S.op("ACT", lambda e: e.activation(out=out_ap, in_=in_ap, func=AF.Copy), reads=reads, writes=writes)
                else:
                    S.op("DVE", lambda e: e.tensor_copy(out=out_ap, in_=in_ap), reads=reads, writes=writes)

            def rotary(p, col0):
                ps_ = pj[p][:, :].ap[0][0]
                x1 = bcast_ap(pj[p][:, 0:1], [[ps_, 128], [128, 4], [1, 64]])
                x2 = bcast_ap(pj[p][:, 64:65], [[ps_, 128], [128, 4], [1, 64]])
                cosv = bcast_ap(cst[:, cs, 0:1], [[256, 128], [0, 4], [1, 64]])
                sinv = bcast_ap(cst[:, cs, 64:65], [[256, 128], [0, 4], [1, 64]])
                t1 = bcast_ap(th[:, 0:1], [[512, 128], [64, 4], [1, 64]])
                t2 = bcast_ap(th[:, 256:257], [[512, 128], [64, 4], [1, 64]])
                o1 = bcast_ap(rot[:, col0:col0 + 1], [[1024, 128], [128, 4], [1, 64]])
                o2 = bcast_ap(rot[:, col0 + 64:col0 + 65], [[1024, 128], [128, 4], [1, 64]])
                rd = [pj_r[p], cst_r[cs]]
                oh = rot_r[col0 // 512]
                S.op("DVE", lambda e: e.tensor_tensor(out=t1, in0=x1, in1=cosv, op=ALU.mult), reads=rd, writes=[thA_r])
                S.op("DVE", lambda e: e.tensor_tensor(out=t2, in0=x2, in1=sinv, op=ALU.mult), reads=rd, writes=[thB_r])
                S.op("DVE", lambda e: e.tensor_tensor(out=o1, in0=t1, in1=t2, op=ALU.subtract), reads=[thA_r, thB_r], writes=[oh[0]])
                S.op("DVE", lambda e: e.tensor_tensor(out=t1, in0=x1, in1=sinv, op=ALU.mult), reads=rd, writes=[thA_r])
                S.op("DVE", lambda e: e.tensor_tensor(out=t2, in0=x2, in1=cosv, op=ALU.mult), reads=rd, writes=[thB_r])
                S.op("DVE", lambda e: e.tensor_tensor(out=o2, in0=t1, in1=t2, op=ALU.add), reads=[thA_r, thB_r], writes=[oh[1]])

            prq = proj(2056, 512)
            rotary(prq, 0)
            yield
            prk = proj(2568, 512)
            rotary(prk, 512)
            for h in range(4):
                S.op("DVE", lambda e, h=h: e.tensor_scalar(out=kd[:, h * 128:(h + 1) * 128], in0=rot[:, 512 + h * 128:512 + (h + 1) * 128],
                                                           scalar1=kdec[:, h:h + 1], scalar2=None, op0=ALU.mult),
                     reads=rot_r[1] + [cr["kdec"]], writes=[kd_r[h]])
            yield
            prv = proj(3080, 512)
            copy_from_psum(rv[:, :], pj[prv][:, :], [pj_r[prv]], [rv_r])
            prz = proj(3592, 512)
            lag()
            S.op("ACT", lambda e: e.activation(out=th[:, :], in_=pj[prz][:, :], func=AF.Tanh, scale=0.5), reads=[pj_r[prz]], writes=th_r)
            S.op("DVE", lambda e: e.scalar_tensor_tensor(out=gret[:, :], in0=th[:, :], scalar=1.0, in1=pj[prz][:, :],
                                                         op0=ALU.add, op1=ALU.mult), reads=th_r + [pj_r[prz]], writes=[gret_r])
            yield
            pt2 = nxt("ptr", 2)
            transposes(rot, rot_r, pt2)
            S.op("DVE", lambda e: e.tensor_copy(out=rqT[:, :, :], in_=ptr[pt2][:, 0:4, :]), reads=[ptr_r[pt2]], writes=[rqT_r])
            S.op("DVE", lambda e: e.tensor_tensor(out=rqdT[:, :, :], in0=ptr[pt2][:, 0:4, :], in1=qdec[:, :, :], op=ALU.mult),
                 reads=[ptr_r[pt2], cr["qdec"]], writes=[rqdT_r])
            S.op("DVE", lambda e: e.tensor_copy(out=rkT[:, :, :], in_=ptr[pt2][:, 4:8, :]), reads=[ptr_r[pt2]], writes=[rkT_r])
            yield
            p = proj(2048, 8)
            ecol = 32 + 8 * (i % 2)
            S.op("DVE", lambda e, p=p: e.tensor_tensor(out=smB[:, ecol:ecol + 8], in0=pj[p][:, 0:8], in1=fbt[:, :], op=ALU.add),
                 reads=[pj_r[p], cr["fbt"]], writes=[lfe_r[i % 2]])
            pq = proj(0, 512)
            copy_from_psum(qk[:, 0:512], pj[pq][:, :], [pj_r[pq]], qk_r[0])
            lag()
            S.op("ACT", lambda e: e.activation(out=smB[:, ecol:ecol + 8], in_=smB[:, ecol:ecol + 8], func=AF.Exp, scale=-1.0),
                 reads=[lfe_r[i % 2]], writes=[lfe_r[i % 2]])
            if i % 2 == 1:
                S.op("ACT", lambda e: e.activation(out=lf[:, :], in_=smB[:, 32:48], func=AF.Ln, bias=1.0),
                     reads=lfe_r, writes=[lf_r])
            pk = proj(512, 512)
            S.op("DVE", lambda e: e.tensor_copy(out=qk[:, 512:1024], in_=pj[pk][:, :]), reads=[pj_r[pk]], writes=qk_r[1])
            yield
            if i == 0:
                S.op("POOL", lambda e: e.memset(st_f[:, :, :].rearrange("p a b -> p (a b)"), 0.0), writes=st_f_r)
                S.op("POOL", lambda e: e.memset(st_b[:, :, :].rearrange("p a b -> p (a b)"), 0.0), writes=st_b_r)

            def ret_pair(hp):
                hs = (2 * hp, 2 * hp + 1)
                RBs = {}
                for h in hs:
                    rb = nxt("pj", 2)
                    RBs[h] = (pj[rb], pj_r[rb])
                sds = {}
                for h in hs:
                    RB, RB_r = RBs[h]
                    S.op("PE", lambda e, h=h, RB=RB: e.matmul(RB[:, 0:128], lhsT=rkT[:, h, :], rhs=rqT[:, h, :], start=True, stop=True),
                         reads=[rkT_r, rqT_r], writes=[RB_r])
                    S.op("PE", lambda e, h=h, RB=RB: e.matmul(RB[:, 256:384], lhsT=kd[:, h * 128:(h + 1) * 128], rhs=rv[:, h * 128:(h + 1) * 128],
                                                              start=True, stop=True), reads=[kd_r[h], rv_r], writes=[RB_r])
                for h in hs:
                    RB, RB_r = RBs[h]
                    sd = nxt("Sd", 2)
                    sds[h] = sd
                    S.op("DVE", lambda e, h=h, RB=RB, sd=sd: e.tensor_tensor(out=Sd[:, sd, :], in0=RB[:, 0:128], in1=dT[:, h, :], op=ALU.mult),
                         reads=[RB_r, cr["dT"]], writes=[Sd_r[sd]])
                    S.op("DVE", lambda e, h=h, RB=RB: e.scalar_tensor_tensor(out=st_f[:, h, :], in0=st_f[:, h, :], scalar=g128[h], in1=RB[:, 256:384],
                                                                             op0=ALU.mult, op1=ALU.add),
                         reads=[st_f_r[h], RB_r], writes=[st_f_r[h]])
                for h in hs:
                    RB, RB_r = RBs[h]
                    sd = sds[h]
                    S.op("PE", lambda e, h=h, RB=RB, sd=sd: e.matmul(RB[:, 128:256], lhsT=Sd[:, sd, :], rhs=rv[:, h * 128:(h + 1) * 128],
                                                                     start=True, stop=False),
                         reads=[Sd_r[sd], rv_r], writes=[RB_r])
                    S.op("PE", lambda e, h=h, RB=RB: e.matmul(RB[:, 128:256], lhsT=rqdT[:, h, :], rhs=st_b[:, h, :], start=False, stop=True),
                         reads=[rqdT_r, st_b_r[h]], writes=[RB_r])
                for h in hs:
                    RB, RB_r = RBs[h]
                    S.op("DVE", lambda e, h=h, RB=RB: e.bn_stats(out=bn[:, h, 0:6], in_=RB[:, 128:256]), reads=[RB_r], writes=[bn_r[h]])
                    S.op("DVE", lambda e, h=h: e.bn_aggr(out=bn[:, h, 6:8], in_=bn[:, h, 0:6]), reads=[bn_r[h]], writes=[bn_r[h]])
                    S.op("DVE", lambda e, h=h: e.tensor_scalar(out=smB[:, 24 + h:25 + h], in0=bn[:, h, 7:8], scalar1=GN_EPS, scalar2=None,
                                                               op0=ALU.add), reads=[bn_r[h]], writes=[gn_r[h]])
                for h in hs:
                    S.op("POOL", lambda e, h=h: e.tensor_tensor(out=smB[:, 16 + h:17 + h], in0=smB[:, 24 + h:25 + h],
                                                                in1=negh[:, :], op=ALU.pow), reads=[gn_r[h], cr["negh"]], writes=[gn_r[h]])
                    S.op("POOL", lambda e, h=h: e.tensor_copy(out=st_b[:, h, :], in_=st_f[:, h, :]), reads=[st_f_r[h]], writes=[st_b_r[h]])
                for h in hs:
                    S.op("DVE", lambda e, h=h: e.scalar_tensor_tensor(out=smB[:, 20 + h:21 + h], in0=bn[:, h, 6:7], scalar=-1.0,
                                                                      in1=smB[:, 16 + h:17 + h], op0=ALU.mult, op1=ALU.mult),
                         reads=[bn_r[h], gn_r[h]], writes=[gn_r[h]])
                for h in hs:
                    RB, RB_r = RBs[h]
                    if evac == "ACT":
                        S.op("ACT", lambda e, h=h, RB=RB: e.activation(out=RB[:, 128:256], in_=RB[:, 128:256], func=AF.Identity,
                                                                       scale=smB[:, 16 + h:17 + h], bias=smB[:, 20 + h:21 + h]),
                             reads=[RB_r, gn_r[h]], writes=[RB_r])
                    else:
                        S.op("DVE", lambda e, h=h, RB=RB: e.tensor_scalar(out=RB[:, 128:256], in0=RB[:, 128:256], scalar1=smB[:, 16 + h:17 + h],
                                                                          scalar2=smB[:, 20 + h:21 + h], op0=ALU.mult, op1=ALU.add),
                             reads=[RB_r, gn_r[h]], writes=[RB_r])
                for h in hs:
                    RB, RB_r = RBs[h]
                    S.op("DVE", lambda e, h=h, RB=RB: e.tensor_tensor(out=mixed[g % 4][:, 512 + h * 128:512 + (h + 1) * 128], in0=RB[:, 128:256],
                                                                      in1=gret[:, h * 128:(h + 1) * 128], op=ALU.mult),
                         reads=[RB_r, gret_r], writes=[mixr_r[g % 4][h]])

            ret_pair(0)
            yield
            cA, cM = carry_r[0], carry_r[1]
            if i == 0:
                S.op("DVE", lambda e: e.memset(carry[:, 0, :], 0.0), writes=[cA])
            if i % 2 == 1:
                pc_ = nxt("pj", 2)
                for t_ in range(2):
                    S.op("PE", lambda e, t_=t_: e.matmul(pj[pc_][:, 16 * t_:16 * t_ + 8], lhsT=triu[:, :], rhs=lf[:, 8 * t_:8 * t_ + 8],
                                                         start=True, stop=True), reads=[lf_r, cr["triu"]], writes=[pj_r[pc_]])
                    S.op("PE", lambda e, t_=t_: e.matmul(pj[pc_][:, 16 * t_ + 8:16 * t_ + 16], lhsT=onesf[:, :], rhs=lf[:, 8 * t_:8 * t_ + 8],
                                                         start=True, stop=True), reads=[lf_r, cr["ones"]], writes=[pj_r[pc_]])
                S.op("DVE", lambda e: e.tensor_tensor(out=a_all[:, i - 1, :], in0=pj[pc_][:, 0:8], in1=carry[:, 0, :], op=ALU.add),
                     reads=[pj_r[pc_], cA], writes=[aall_r])
                S.op("DVE", lambda e: e.tensor_tensor(out=carry[:, 1, :], in0=pj[pc_][:, 8:16], in1=carry[:, 0, :], op=ALU.add),
                     reads=[pj_r[pc_], cA], writes=[cM])
                S.op("DVE", lambda e: e.tensor_tensor(out=a_all[:, i, :], in0=pj[pc_][:, 16:24], in1=carry[:, 1, :], op=ALU.add),
                     reads=[pj_r[pc_], cM], writes=[aall_r])
                S.op("DVE", lambda e: e.tensor_tensor(out=carry[:, 0, :], in0=pj[pc_][:, 24:32], in1=carry[:, 1, :], op=ALU.add),
                     reads=[pj_r[pc_], cM], writes=[cA])
            pt = nxt("ptr", 2)
            transposes(qk, qk_r, pt)
            for e_ in range(2):
                S.op("DVE", lambda e, e_=e_: e.tensor_copy(out=QT[pm][e_ * 64:(e_ + 1) * 64, :, e_, (g % 2) * 128:(g % 2 + 1) * 128],
                                                           in_=ptr[pt][e_ * 64:(e_ + 1) * 64, 0:4, :]),
                     reads=[ptr_r[pt]], writes=[QT_r[pm][g % 2][e_]])
            S.op("DVE", lambda e: e.tensor_copy(out=KT[:, :, i * 128:(i + 1) * 128], in_=ptr[pt][:, 4:8, :]),
                 reads=[ptr_r[pt]], writes=[KT_r[i]])
            yield
            ret_pair(1)
            yield
            if i % 2 == 1:
                S.op("DVE", lambda e: e.tensor_scalar(out=smB[:, 8:16], in0=carry[:, 1, :], scalar1=-1.0, scalar2=None, op0=ALU.mult),
                     reads=[cM], writes=[aref_r])
                for h in range(FOX_HEADS):
                    S.op("DVE", lambda e, h=h: e.tensor_scalar(out=biasm[pm][:, 0:i + 1, h], in0=a_all[:, 0:i + 1, h],
                                                               scalar1=smB[:, 8 + h:9 + h], scalar2=None, op0=ALU.add),
                         reads=[aref_r, aall_r], writes=[biasm_r[pm][h]])
            pv = proj(1024, 512)
            copy_from_psum(VA[:, i, :, 0:64], pj[pv][:, :].rearrange("p (h d) -> p h d", h=FOX_HEADS), [pj_r[pv]], [VA_r[i]])
            yield
            pz = proj(1536, 512)
            lag()
            S.op("ACT", lambda e: e.activation(out=th[:, :], in_=pj[pz][:, :], func=AF.Tanh, scale=0.5), reads=[pj_r[pz]], writes=th_r)
            S.op("DVE", lambda e: e.scalar_tensor_tensor(out=gfox[g % 4][:, :], in0=th[:, :], scalar=1.0, in1=pj[pz][:, :],
                                                         op0=ALU.add, op1=ALU.mult), reads=th_r + [pj_r[pz]], writes=[gfox_r[g % 4]])

        def stage_C(mg):
            g0 = 2 * mg
            i0 = g0 % NT
            i1 = i0 + 1
            pm = mg % 2
            blocks = [(h, j) for h in range(FOX_HEADS) for j in range(i1 + 1)]
            groups = [blocks[k:k + 2] for k in range(0, len(blocks), 2)]
            info = {}

            def emit_S(gi):
                sb_ = nxt("pS", 2)
                bank = pS[sb_][:, :, :].rearrange("p a b -> p (a b)")
                for q, (h, j) in enumerate(groups[gi]):
                    p_, e_ = h // 2, h % 2
                    dst = bank[:, q * 256:(q + 1) * 256]
                    lo = 128 if j == i1 else 0
                    diag = (j >= i0)
                    S.op("PE", lambda e, dst=dst, p_=p_, e_=e_, j=j, lo=lo, diag=diag: e.matmul(
                        dst[:, lo:256], lhsT=KT[:, p_, j * 128:(j + 1) * 128],
                        rhs=QT[pm][:, p_, e_, lo:256], start=True, stop=not diag),
                         reads=[KT_r[j], QT_r[pm][0][e_], QT_r[pm][1][e_]], writes=[pS_r[sb_]])
                    if diag:
                        S.op("PE", lambda e, dst=dst, lo=lo: e.matmul(dst[:, lo:lo + 128], lhsT=ident[:, :], rhs=maskneg[:, :],
                                                                     start=False, stop=True),
                             reads=[cr["ident"], cr["mask"]], writes=[pS_r[sb_]])
                for q, (h, j) in enumerate(groups[gi]):
                    dst = bank[:, q * 256:(q + 1) * 256]
                    lo = 128 if j == i1 else 0
                    pt_ = nxt("PT", NPT)
                    S.op("ACT", lambda e, dst=dst, pt_=pt_, h=h, j=j, lo=lo: e.activation(out=PT[:, pt_, lo:256], in_=dst[:, lo:256], func=AF.Exp,
                                                                                        scale=0.125, bias=biasm[pm][:, j, h:h + 1]),
                         reads=[pS_r[sb_], biasm_r[pm][h]], writes=[PT_r[pt_]])
                    info[(h, j)] = pt_

            def emit_PV(gi):
                for (h, j) in groups[gi]:
                    pt_ = info[(h, j)]
                    for q in range(2):
                        iq = i0 + q
                        if j > iq:
                            continue
                        ob = h % 2
                        S.op("PE", lambda e, pt_=pt_, h=h, j=j, q=q, iq=iq, ob=ob: e.matmul(
                            pO[ob][:, q * 128:q * 128 + 65], lhsT=PT[:, pt_, q * 128:(q + 1) * 128], rhs=VA[:, j, h, :],
                            start=(j == 0 and q == 0), stop=(j == iq), skip_group_check=True),
                             reads=[PT_r[pt_], VA_r[j]], writes=[pO_r[ob]])
                        if j == i1 and q == 1:
                            for qq in range(2):
                                mb = (g0 + qq) % 4
                                S.op("DVE", lambda e, h=h, qq=qq, ob=ob: e.reciprocal(out=smC[:, qq, h:h + 1],
                                                                                     in_=pO[ob][:, qq * 128 + 64:qq * 128 + 65]),
                                     reads=[pO_r[ob]], writes=[rl_r[qq]])
                            for qq in range(2):
                                mb = (g0 + qq) % 4
                                S.op("DVE", lambda e, h=h, qq=qq, mb=mb, ob=ob: e.scalar_tensor_tensor(
                                    out=mixed[mb][:, h * 64:(h + 1) * 64], in0=pO[ob][:, qq * 128:qq * 128 + 64],
                                    scalar=smC[:, qq, h:h + 1], in1=gfox[mb][:, h * 64:(h + 1) * 64], op0=ALU.mult, op1=ALU.mult),
                                     reads=[pO_r[ob], rl_r[qq], gfox_r[mb]], writes=[mixf_r[mb]])

            ng = len(groups)
            for gi in range(ng + 1):
                if gi < ng:
                    emit_S(gi)
                if gi - 1 >= 0:
                    emit_PV(gi - 1)
                yield

        def stage_D(g):
            pb = g % 4
            X, X_r = big[1], big_r[1]
            S.dma(lambda e: e.dma_start(out=X[:, :], in_=x_d[g * 128:(g + 1) * 128, :]), writes=[X_r])
            pt = nxt("ptr", 2)
            for c in range(8):
                S.op("PE", lambda e, c=c: e.transpose(out=ptr[pt][:, c, :], in_=mixed[pb][:, c * 128:(c + 1) * 128], identity=ident[:, :]),
                     reads=[mixf_r[pb]] + mixr_r[pb] + [cr["ident"]], writes=[ptr_r[pt]])
                tick()
            S.op("DVE", lambda e: e.tensor_scalar(out=mixT[:, :, :], in0=ptr[pt][:, :, :], scalar1=0.5, scalar2=None, op0=ALU.mult),
                 reads=[ptr_r[pt]], writes=[mixT_r])
            yield
            ph = []
            for half in range(2):
                p = nxt("pj", 2)
                for kc in range(8):
                    S.op("PE", lambda e, kc=kc, p=p, half=half: e.matmul(pj[p][:, :], lhsT=mixT[:, kc, :],
                                                                         rhs=WO[:, kc, half * 512:(half + 1) * 512],
                                                                         start=(kc == 0), stop=(kc == 7)),
                         reads=[mixT_r, WO_r[kc]], writes=[pj_r[p]])
                    tick()
                ph.append(p)
                lag(1)
                S.op("ACT", lambda e, p=p, half=half: e.activation(out=junk[:, :], in_=pj[p][:, :], func=AF.Square,
                                                                   accum_out=smD[:, half:half + 1]),
                     reads=[pj_r[p]], writes=[junk_r[half], ssD_r[half]])
            S.op("DVE", lambda e: e.tensor_tensor(out=smD[:, 2:3], in0=smD[:, 0:1], in1=smD[:, 1:2], op=ALU.add),
                 reads=ssD_r, writes=[msD_r])
            S.op("DVE", lambda e: e.tensor_scalar(out=smD[:, 2:3], in0=smD[:, 2:3], scalar1=1.0 / D_MODEL,
                                                  scalar2=NORM_EPS, op0=ALU.mult, op1=ALU.add), reads=[msD_r], writes=[msD_r])
            S.op("POOL", lambda e: e.tensor_tensor(out=smD[:, 3:4], in0=smD[:, 2:3], in1=negh[:, :], op=ALU.pow),
                 reads=[msD_r, cr["negh"]], writes=[rstdD_r])
            for half in range(2):
                p = ph[half]
                S.op("DVE", lambda e, p=p, half=half: e.scalar_tensor_tensor(out=pj[p][:, :], in0=pj[p][:, :], scalar=smD[:, 3:4],
                                                                             in1=g2t[:, half * 512:(half + 1) * 512],
                                                                             op0=ALU.mult, op1=ALU.mult),
                     reads=[pj_r[p], rstdD_r, cr["g2t"]], writes=[pj_r[p]])
                S.op("DVE", lambda e, p=p, half=half: e.tensor_tensor(out=X[:, half * 512:(half + 1) * 512], in0=pj[p][:, :],
                                                                      in1=X[:, half * 512:(half + 1) * 512], op=ALU.add),
                     reads=[pj_r[p], X_r], writes=[X_r])
            if fill["fn"] is not None:
                fill["fn"](4)
            yield
            S.dma(lambda e: e.dma_start(out=y_d[g * 128:(g + 1) * 128, :], in_=X[:, :]), reads=[X_r])

        NG = NSEQ * NT
        assert NT % 2 == 0
        catt = None
        d_queue = []
        it = 0
        next_pair = 0
        ROUNDS = 12

        def c_steps(n):
            nonlocal catt
            for _ in range(n):
                if catt is None:
                    return
                try:
                    next(catt[1])
                    catt[4] += 1
                except StopIteration:
                    d_queue.extend([2 * catt[0], 2 * catt[0] + 1])
                    catt = None

        while True:
            gens = []
            if 0 <= it - 1 < NG:
                gens.append(stage_B(it - 1))
            if it < NG:
                gens.append(stage_A(it))
            if d_queue:
                gens.append(stage_D(d_queue.pop(0)))
            if catt is None and next_pair * 2 + 1 <= it - 2 and next_pair * 2 < NG:
                i1 = (2 * next_pair) % NT + 1
                catt = [next_pair, stage_C(next_pair), it, 4 * (i1 + 1) + 1, 0]
                next_pair += 1
            if it - 1 > 0 and (it - 1) % NT == 0 and it - 1 < NG + 1:
                c_steps(10 ** 6)
            k = 1
            fill["fn"] = c_steps
            fill["rate"] = 0.0
            fill["credit"] = 0.0
            if catt is not None:
                remaining = catt[3] - catt[4]
                target = remaining if (it - catt[2] >= 1 or it >= NG) else (catt[3] + 1) // 2
                fill["rate"] = max(0.0, (target - ROUNDS - 12)) / 130.0
                k = 1
            while gens:
                first = True
                for gen in list(gens):
                    try:
                        next(gen)
                    except StopIteration:
                        gens.remove(gen)
                    if first:
                        c_steps(k)
                        first = False
            if catt is not None and (it - catt[2] >= 1 or it >= NG):
                c_steps(10 ** 6)
            it += 1
            if it > NG + 1 and catt is None and not d_queue and next_pair * 2 >= NG:
                break

        sem_names = ["PE", "ACT", "DVE", "POOL"]
        sems = {n: es.enter_context(nc.semaphore(f"s_{n}")) for n in sem_names}
        dsems = [es.enter_context(nc.semaphore(f"d_{k}")) for k in range(NDS)]
        qsems = [es.enter_context(nc.semaphore(f"q_{k}")) for k in range(S.n_qdma)]
        block = es.enter_context(nc.Block())
        S.emit(nc, block, sems, dsems, qsems)
    return nc


def kernel(x, pre_norm_gain, w_in, fox_forget_bias, w_out, post_norm_gain):
    x = np.asarray(x, np.float32)
    B, SEQ, D = x.shape
    NSEQ = B // N_CORES
    NT = SEQ // 128
    consts, g128 = make_consts(SEQ)
    nc = build_nc(NSEQ, NT, g128)
    shared = dict(
        w_in=np.ascontiguousarray(np.asarray(w_in, np.float32)[0]),
        w_out=np.ascontiguousarray(np.asarray(w_out, np.float32)[0]),
        g1t=np.ascontiguousarray(np.asarray(pre_norm_gain, np.float32)[0].reshape(8, 128).T),
        g2=np.ascontiguousarray(np.asarray(post_norm_gain, np.float32)[0].reshape(1, D)),
        fb=np.ascontiguousarray(np.asarray(fox_forget_bias, np.float32)[0].reshape(1, FOX_HEADS)),
        **consts,
    )
    in_maps = []
    for c in range(N_CORES):
        m = dict(shared)
        m["x"] = np.ascontiguousarray(x[c * NSEQ:(c + 1) * NSEQ].reshape(NSEQ * SEQ, D))
        in_maps.append(m)
    res = run_bass_kernel_spmd(nc, in_maps, core_ids=list(range(N_CORES)))
    out = np.concatenate([np.asarray(r["y"], np.float32).reshape(NSEQ, SEQ, D) for r in res.results], axis=0)
    return out
```

```python
import math
import os
from contextlib import ExitStack

import numpy as np
import ml_dtypes
import concourse.bass as bass
import concourse.mybir as mybir
from concourse.bass_utils import run_bass_kernel_spmd

F32 = mybir.dt.float32
BF16 = mybir.dt.bfloat16
ALU = mybir.AluOpType
AF = mybir.ActivationFunctionType

D_MODEL = 1024
D_IN = 4104
FOX_HEADS = 8
RET_HEADS = 4
NORM_EPS = 1e-6
GN_EPS = 1e-5
N_CORES = 8

SAME_ENGINE_SYNC = bool(int(os.environ.get("KSES", "1")))

STAGE = float(os.environ.get('KSTAGE', '99'))
NDS = 16
DUMP = bool(int(os.environ.get('KDUMP', '0')))


class Res:
    __slots__ = ("name", "last_w", "readers")

    def __init__(self, name):
        self.name = name
        self.last_w = None
        self.readers = []


class Op:
    __slots__ = ("fn", "deps", "dma_k", "needed")

    def __init__(self, fn, deps, dma_k=None):
        self.fn = fn
        self.deps = deps
        self.dma_k = dma_k
        self.needed = False


class Sched:
    ENGS = ("SP", "PE", "ACT", "DVE", "POOL")

    def __init__(self):
        self.ops = {e: [] for e in self.ENGS}
        self.n_dma = 0
        self.n_qdma = 0

    def op(self, eng, fn, reads=(), writes=()):
        deps = set()
        for r in reads:
            if r.last_w is not None:
                deps.add(r.last_w)
        for w in writes:
            if w.last_w is not None:
                deps.add(w.last_w)
            for rd in w.readers:
                deps.add(rd)
        idx = len(self.ops[eng])
        me = (eng, idx)
        deps.discard(me)
        self.ops[eng].append(Op(fn, deps))
        for r in reads:
            r.readers.append(me)
        for w in writes:
            w.last_w = me
            w.readers = []
        return me

    def dma(self, fn, reads=(), writes=(), eng="SP"):
        me = self.op(eng, fn, reads, writes)
        if eng == "SP":
            self.ops[eng][me[1]].dma_k = self.n_dma
            self.n_dma += 1
        else:
            self.ops[eng][me[1]].dma_k = -1 - self.n_qdma
            self.n_qdma += 1
        return me

    def emit(self, nc, block, sems, dsems, qsems):
        for e in self.ENGS:
            for o in self.ops[e]:
                for (e2, i2) in o.deps:
                    if self.ops[e2][i2].dma_k is not None:
                        continue
                    if e2 == e and (e == "PE" or not SAME_ENGINE_SYNC):
                        continue
                    self.ops[e2][i2].needed = True
        cnt = {}
        for e in self.ENGS:
            if e == "SP":
                continue
            c = 0
            arr = []
            for o in self.ops[e]:
                if o.needed:
                    c += 1
                arr.append(c)
            cnt[e] = arr

        def run(e, engobj):
            waited = {}
            for idx, o in enumerate(self.ops[e]):
                need = {}
                for (e2, i2) in o.deps:
                    if self.ops[e2][i2].dma_k is not None:
                        k = self.ops[e2][i2].dma_k
                        if k < 0:
                            key = ("Q", -1 - k)
                            val = 16
                        else:
                            key = ("D", k % NDS)
                            val = 16 * (k // NDS + 1)
                    else:
                        if e2 == e and (e == "PE" or not SAME_ENGINE_SYNC):
                            continue
                        key = ("C", e2)
                        val = cnt[e2][i2]
                    if val > need.get(key, 0):
                        need[key] = val
                if o.dma_k is not None and o.dma_k >= NDS:
                    key = ("D", o.dma_k % NDS)
                    val = 16 * (o.dma_k // NDS)
                    if val > need.get(key, 0):
                        need[key] = val
                for key, val in need.items():
                    if waited.get(key, 0) >= val:
                        continue
                    waited[key] = val
                    s = dsems[key[1]] if key[0] == "D" else (qsems[key[1]] if key[0] == "Q" else sems[key[1]])
                    engobj.wait_ge(s, val)
                    if DUMP:
                        print("   ", e, idx, "WAIT", key, val)
                ins = o.fn(engobj)
                if DUMP:
                    print(e, idx, "deps", sorted(o.deps), "inc" if o.needed else "", o.dma_k)
                if o.dma_k is not None and o.dma_k < 0:
                    ins.then_inc(qsems[-1 - o.dma_k], 16)
                elif o.dma_k is not None:
                    ins.then_inc(dsems[o.dma_k % NDS], 16)
                elif o.needed:
                    ins.then_inc(sems[e], 1)
            if e == "SP":
                for s in range(min(NDS, self.n_dma)):
                    last_k = ((self.n_dma - 1 - s) // NDS) * NDS + s
                    engobj.wait_ge(dsems[s], 16 * (last_k // NDS + 1))

        @block.sync
        def _(e):
            run("SP", e)

        @block.tensor
        def _(e):
            run("PE", e)

        @block.scalar
        def _(e):
            run("ACT", e)

        @block.vector
        def _(e):
            run("DVE", e)

        @block.gpsimd
        def _(e):
            run("POOL", e)


def bcast_ap(ap, dims):
    return bass.AP(tensor=ap.tensor, offset=ap.offset, ap=[list(d) for d in dims])


def make_consts(seq_len):
    lg = np.log1p(-np.exp(np.linspace(math.log(1.0 / 32), math.log(1.0 / 512), RET_HEADS))).astype(np.float32)
    lg64 = lg.astype(np.float64)
    ident = np.eye(128, dtype=np.float32).astype(ml_dtypes.bfloat16)
    s_idx = np.arange(128)[:, None]
    t_idx = np.arange(128)[None, :]
    maskneg = np.where(s_idx <= t_idx, 0.0, -30000.0).astype(np.float32).astype(ml_dtypes.bfloat16)
    triu = (s_idx <= t_idx).astype(np.float32)
    j = np.arange(128)[:, None]
    i = np.arange(128)[None, :]
    same = (i // 64) == (j // 64)
    fwd = (j // 64 == 0) & (i // 64 == 1)
    dT = np.zeros((128, RET_HEADS, 128), np.float32)
    for h in range(RET_HEADS):
        d = np.where(same, np.exp(lg64[h] * np.abs(i - j)), np.where(fwd, np.exp(lg64[h] * (i - j)), 0.0))
        dT[:, h, :] = (d * (128.0 ** -0.5)).astype(np.float32)
    il = np.arange(128, dtype=np.float64)
    qdec = np.stack([np.exp(lg64[h] * (il + 1.0)) for h in range(RET_HEADS)], 0).astype(np.float32)
    kdec = np.stack([np.exp(lg64[h] * (127.0 - il)) * (128.0 ** -0.5) for h in range(RET_HEADS)], 1).astype(np.float32)
    g128 = [float(np.exp(lg64[h] * 128.0)) for h in range(RET_HEADS)]
    half = 64
    inv = (1.0 / (10000.0 ** (np.arange(half, dtype=np.float32) / half))).astype(np.float32)
    ang = np.arange(seq_len, dtype=np.float32)[:, None] * inv[None, :]
    cossin = np.concatenate([np.cos(ang), np.sin(ang)], axis=1).astype(np.float32)
    return dict(ident=ident, maskneg=maskneg, triu=triu, dT=dT.reshape(128, RET_HEADS * 128),
                qdec=qdec.reshape(1, RET_HEADS * 128), kdec=kdec, cossin=cossin), g128


def build_nc(NSEQ, NT, g128):
    nc = bass.Bass("TRN2", target_bir_lowering=False, dynamic_dma_scratch_size=4096)
    NTOK = NSEQ * NT * 128
    SEQL = NT * 128
    x_d = nc.dram_tensor("x", [NTOK, D_MODEL], F32, kind="ExternalInput").ap()
    win_d = nc.dram_tensor("w_in", [D_MODEL, D_IN], F32, kind="ExternalInput").ap()
    wout_d = nc.dram_tensor("w_out", [D_MODEL, D_MODEL], F32, kind="ExternalInput").ap()
    g1t_d = nc.dram_tensor("g1t", [128, 8], F32, kind="ExternalInput").ap()
    g2_d = nc.dram_tensor("g2", [1, D_MODEL], F32, kind="ExternalInput").ap()
    fb_d = nc.dram_tensor("fb", [1, FOX_HEADS], F32, kind="ExternalInput").ap()
    ident_d = nc.dram_tensor("ident", [128, 128], BF16, kind="ExternalInput").ap()
    mask_d = nc.dram_tensor("maskneg", [128, 128], BF16, kind="ExternalInput").ap()
    triu_d = nc.dram_tensor("triu", [128, 128], F32, kind="ExternalInput").ap()
    dT_d = nc.dram_tensor("dT", [128, 512], F32, kind="ExternalInput").ap()
    qdec_d = nc.dram_tensor("qdec", [1, 512], F32, kind="ExternalInput").ap()
    kdec_d = nc.dram_tensor("kdec", [128, 4], F32, kind="ExternalInput").ap()
    cs_d = nc.dram_tensor("cossin", [SEQL, 128], F32, kind="ExternalInput").ap()
    y_d = nc.dram_tensor("y", [NTOK, D_MODEL], F32, kind="ExternalOutput").ap()

    S = Sched()
    es = ExitStack()

    def sb(name, shape, dt):
        return es.enter_context(nc.sbuf_tensor(name, shape, dt))

    def ps(name, shape, dt):
        return es.enter_context(nc.psum_tensor(name, shape, dt))

    with es:
        W = sb("W", [128, 8, D_IN], BF16)
        WO = sb("WO", [128, 8, D_MODEL], BF16)
        KT = sb("KT", [128, 4, SEQL], BF16)
        VA = sb("VA", [128, NT, FOX_HEADS, 65], BF16)
        KT_r = [Res(f"KT{t}") for t in range(NT)]
        VA_r = [Res(f"VA{t}") for t in range(NT)]
        a_all = sb("a_all", [128, NT, FOX_HEADS], F32)
        aall_r = Res("a_all")
        g2t = sb("g2t", [128, D_MODEL], F32)
        fbt = sb("fbt", [128, FOX_HEADS], F32)
        g1t = sb("g1t_s", [128, 8], F32)
        NBIG = 2
        big = [sb(f"big{i}", [128, 1024], F32) for i in range(NBIG)]
        big_r = [Res(f"big{i}") for i in range(NBIG)]
        xn = sb("xn", [128, 1024], BF16)
        xn_r = Res("xn")
        junk = sb("junk", [128, 512], BF16)
        _jr = Res("junk")
        junk_r = [_jr, _jr]
        uT = [sb(f"uT{k}", [128, 8, 128], BF16) for k in range(2)]
        uT_r = [[Res(f"uT{k}_{c}") for c in range(8)] for k in range(2)]
        qk = sb("qk", [128, 1024], BF16)
        qk_r = [[Res(f"qk{a}_{b}") for b in range(2)] for a in range(2)]
        QT = [sb(f"QT{k}", [128, 4, 2, 256], BF16) for k in range(2)]
        QT_r = [[[Res(f"QT{k}_{q}_{e}") for e in range(2)] for q in range(2)] for k in range(2)]
        gfox = [sb(f"gfox{k}", [128, 512], F32) for k in range(4)]
        gfox_r = [Res(f"gfox{k}") for k in range(4)]
        gret = sb("gret", [128, 512], F32)
        gret_r = Res("gret")
        th = sb("th", [128, 512], F32)
        thA_r, thB_r = Res("thA"), Res("thB")
        th_r = [thA_r, thB_r]
        rot, rot_r = qk, qk_r
        kd = sb("kd", [128, 512], BF16)
        kd_r = [Res(f"kd{h}") for h in range(4)]
        rqT = sb("rqT", [128, 4, 128], BF16)
        rkT = sb("rkT", [128, 4, 128], BF16)
        rqdT = sb("rqdT", [128, 4, 128], BF16)
        rqT_r, rkT_r, rqdT_r = Res("rqT"), Res("rkT"), Res("rqdT")
        rv = sb("rv", [128, 512], BF16)
        rv_r = Res("rv")
        st_f = sb("st_f", [128, 4, 128], F32)
        st_b = sb("st_b", [128, 4, 128], BF16)
        st_f_r = [Res(f"stf{h}") for h in range(4)]
        st_b_r = [Res(f"stb{h}") for h in range(4)]
        NPT = 6
        PT = sb("PT", [128, NPT, 256], BF16)
        PT_r = [Res(f"PT{i}") for i in range(NPT)]
        Sd = sb("Sd", [128, 2, 128], BF16)
        Sd_r = [Res("Sd0"), Res("Sd1")]
        mixed = [sb(f"mixed{k}", [128, 1024], BF16) for k in range(4)]
        mixf_r = [Res(f"mixf{k}") for k in range(4)]
        mixr_r = [[Res(f"mixr{k}_{h}") for h in range(4)] for k in range(4)]
        mixT = sb("mixT", [128, 8, 128], BF16)
        mixT_r = Res("mixT")
        ident = sb("ident_s", [128, 128], BF16)
        maskneg = sb("mask_s", [128, 128], BF16)
        triu = sb("triu_s", [128, 128], F32)
        onesf = sb("onesf", [128, 128], F32)
        dT = sb("dT_s", [128, 4, 128], F32)
        qdec = sb("qdec_s", [128, 4, 128], F32)
        kdec = sb("kdec_s", [128, 4], F32)
        cst = sb("cst", [128, 2, 128], F32)
        cst_r = [Res("cs0"), Res("cs1")]
        smA = sb("smA", [128, 8], F32)
        smB = sb("smB", [128, 64], F32)
        smC = sb("smC", [128, 2, 8], F32)
        smD = sb("smD", [128, 8], F32)
        negh = sb("negh", [128, 1], F32)
        bn = sb("bn", [128, 4, 8], F32)
        biasm = [sb(f"biasm{k}", [128, NT, FOX_HEADS], F32) for k in range(2)]
        biasm_r = [[Res(f"biasm{k}_{h}") for h in range(FOX_HEADS)] for k in range(2)]
        carry = sb("carry", [128, 2, FOX_HEADS], F32)
        lf = sb("lf", [128, 2 * FOX_HEADS], F32)
        lfe_r = [Res("lfe0"), Res("lfe1")]
        W_r = [[Res(f"W{kc}_{pc}") for pc in range(4)] for kc in range(8)]
        WO_r = [Res(f"WO{kc}") for kc in range(8)]
        cr = {n: Res("c_" + n) for n in ["ident", "mask", "triu", "dT", "qdec", "kdec", "g1t", "g2t", "fbt", "ones", "negh"]}
        ssA_r, msA_r, rstdA_r = Res("ssA"), Res("msA"), Res("rstdA")
        ssD_r, msD_r, rstdD_r = [Res("ssD0"), Res("ssD1")], Res("msD"), Res("rstdD")
        rl_r = [Res("rl0"), Res("rl1")]
        aref_r = Res("aref")
        gn_r = [Res(f"gn{h}") for h in range(4)]
        bn_r = [Res(f"bn{h}") for h in range(4)]
        carry_r = [Res("carry0"), Res("carry1")]
        lf_r = Res("lf")

        pj = [ps(f"pj{i}", [128, 512], F32) for i in range(2)]
        pj_r = [Res(f"pj{i}") for i in range(2)]
        ptr = [ps(f"ptr{i}", [128, 8, 128], BF16) for i in range(2)]
        ptr_r = [Res(f"ptr{i}") for i in range(2)]
        pS = [ps(f"pS{i}", [128, 4, 128], F32) for i in range(2)]
        pS_r = [Res(f"pS{i}") for i in range(2)]
        pO = [ps(f"pO{i}", [128, 512], F32) for i in range(2)]
        pO_r = [Res(f"pO{i}") for i in range(2)]

        cnt = {"pj": 0, "ptr": 0, "pS": 0, "PT": 0, "Sd": 0}

        def nxt(k, n):
            v = cnt[k] % n
            cnt[k] += 1
            return v

        S.dma(lambda e: e.dma_start(out=ident[:, :], in_=ident_d[:, :]), writes=[cr["ident"]])
        S.dma(lambda e: e.dma_start(out=maskneg[:, :], in_=mask_d[:, :]), writes=[cr["mask"]])
        S.dma(lambda e: e.dma_start(out=triu[:, :], in_=triu_d[:, :]), writes=[cr["triu"]])
        S.dma(lambda e: e.dma_start(out=dT[:, :, :].rearrange("p a b -> p (a b)"), in_=dT_d[:, :]), writes=[cr["dT"]])
        S.dma(lambda e: e.dma_start(out=qdec[:, :, :].rearrange("p a b -> p (a b)"),
                                    in_=bcast_ap(qdec_d, [[0, 128], [1, 512]])), writes=[cr["qdec"]])
        S.dma(lambda e: e.dma_start(out=kdec[:, :], in_=kdec_d[:, :]), writes=[cr["kdec"]])
        S.dma(lambda e: e.dma_start(out=g1t[:, :], in_=g1t_d[:, :]), writes=[cr["g1t"]])
        S.dma(lambda e: e.dma_start(out=g2t[:, :], in_=bcast_ap(g2_d, [[0, 128], [1, D_MODEL]])), writes=[cr["g2t"]])
        S.dma(lambda e: e.dma_start(out=fbt[:, :], in_=bcast_ap(fb_d, [[0, 128], [1, FOX_HEADS]])), writes=[cr["fbt"]])
        S.op("POOL", lambda e: e.memset(onesf[:, :], 1.0), writes=[cr["ones"]])
        S.op("POOL", lambda e: e.memset(negh[:, :], -0.5), writes=[cr["negh"]])
        S.op("POOL", lambda e: e.memset(VA[:, :, :, :].rearrange("p a b c -> p (a b c)"), 1.0), writes=VA_r)
        for k in range(2):
            S.op("POOL", lambda e, k=k: e.memset(QT[k][:, :, :, :].rearrange("p a b c -> p (a b c)"), 0.0),
                 writes=[QT_r[k][q][e_] for q in range(2) for e_ in range(2)])

        nb = 0
        for pc in (2, 3, 0, 1):
            for kc in range(8):
                c0 = pc * 1026
                S.dma(lambda e, kc=kc, c0=c0: e.dma_start(out=W[:, kc, c0:c0 + 1026], in_=win_d[kc * 128:(kc + 1) * 128, c0:c0 + 1026],
                                                          max_dma_last_dim=4104),
                      writes=[W_r[kc][pc]], eng="POOL")
        for kc in range(8):
            S.dma(lambda e, kc=kc: e.dma_start(out=WO[:, kc, :], in_=wout_d[kc * 128:(kc + 1) * 128, :], max_dma_last_dim=4096),
                  writes=[WO_r[kc]], eng="POOL")

        fill = {"rate": 0.0, "credit": 0.0, "fn": None}

        def tick(n=1):
            if fill["fn"] is None or fill["rate"] <= 0.0:
                return
            fill["credit"] += n * fill["rate"]
            if fill["credit"] >= 1.0:
                k_ = int(fill["credit"])
                fill["credit"] -= k_
                fill["fn"](k_)

        def transposes(src, src_r, pt):
            for c in range(8):
                S.op("PE", lambda e, c=c: e.transpose(out=ptr[pt][:, c, :], in_=src[:, c * 128:(c + 1) * 128], identity=ident[:, :]),
                     reads=(src_r[c // 4] if isinstance(src_r, list) else [src_r]) + [cr["ident"]], writes=[ptr_r[pt]])
                tick()

        def stage_A(g):
            i = g % NT
            X, X_r = big[0], big_r[0]
            S.dma(lambda e: e.dma_start(out=X[:, :], in_=x_d[g * 128:(g + 1) * 128, :]), writes=[X_r])
            S.dma(lambda e: e.dma_start(out=cst[:, g % 2, :], in_=cs_d[i * 128:(i + 1) * 128, :]), writes=[cst_r[g % 2]])
            yield
            S.op("ACT", lambda e: e.activation(out=xn[:, :], in_=X[:, :], func=AF.Square, accum_out=smA[:, 0:1]),
                 reads=[X_r], writes=[xn_r, ssA_r])
            yield
            S.op("DVE", lambda e: e.tensor_scalar(out=smA[:, 1:2], in0=smA[:, 0:1], scalar1=1.0 / D_MODEL,
                                                  scalar2=NORM_EPS, op0=ALU.mult, op1=ALU.add), reads=[ssA_r], writes=[msA_r])
            yield
            S.op("POOL", lambda e: e.tensor_tensor(out=smA[:, 2:3], in0=smA[:, 1:2], in1=negh[:, :], op=ALU.pow),
                 reads=[msA_r, cr["negh"]], writes=[rstdA_r])
            yield
            S.op("DVE", lambda e: e.tensor_scalar(out=xn[:, :], in0=X[:, :], scalar1=smA[:, 2:3], scalar2=None, op0=ALU.mult),
                 reads=[X_r, rstdA_r], writes=[xn_r])
            yield
            pt = nxt("ptr", 2)
            transposes(xn, xn_r, pt)
            for c in range(8):
                S.op("DVE", lambda e, c=c: e.tensor_scalar(out=uT[g % 2][:, c, :], in0=ptr[pt][:, c, :], scalar1=g1t[:, c:c + 1], scalar2=None,
                                                           op0=ALU.mult), reads=[ptr_r[pt], cr["g1t"]], writes=[uT_r[g % 2][c]])

        def stage_B(g):
            i = g % NT
            pb = g % 2
            pm = (g // 2) % 2
            cs = g % 2
            U, U_r = uT[pb], uT_r[pb]

            def proj(c0, ncols):
                p = nxt("pj", 2)
                for kc in range(8):
                    S.op("PE", lambda e, kc=kc: e.matmul(pj[p][:, 0:ncols], lhsT=U[:, kc, :], rhs=W[:, kc, c0:c0 + ncols],
                                                         start=(kc == 0), stop=(kc == 7)),
                         reads=[U_r[kc]] + W_r[kc][c0 // 1026:(c0 + ncols - 1) // 1026 + 1], writes=[pj_r[p]])
                    tick()
                return p

            evac = "ACT" if i < 22 else "DVE"

            def copy_from_psum(out_ap, in_ap, reads, writes):
                if evac == "ACT":
                    S.op("ACT", lambda e: e.activation(out=out_ap, in_=in_ap, func=AF.Copy), reads=reads, writes=writes)
                else:
                    S.op("DVE", lambda e: e.tensor_copy(out=out_ap, in_=in_ap), reads=reads, writes=writes)

            def rotary(p, col0):
                ps_ = pj[p][:, :].ap[0][0]
                x1 = bcast_ap(pj[p][:, 0:1], [[ps_, 128], [128, 4], [1, 64]])
                x2 = bcast_ap(pj[p][:, 64:65], [[ps_, 128], [128, 4], [1, 64]])
                cosv = bcast_ap(cst[:, cs, 0:1], [[256, 128], [0, 4], [1, 64]])
                sinv = bcast_ap(cst[:, cs, 64:65], [[256, 128], [0, 4], [1, 64]])
                t1 = bcast_ap(th[:, 0:1], [[512, 128], [64, 4], [1, 64]])
                t2 = bcast_ap(th[:, 256:257], [[512, 128], [64, 4], [1, 64]])
                o1 = bcast_ap(rot[:, col0:col0 + 1], [[1024, 128], [128, 4], [1, 64]])
                o2 = bcast_ap(rot[:, col0 + 64:col0 + 65], [[1024, 128], [128, 4], [1, 64]])
                rd = [pj_r[p], cst_r[cs]]
                oh = rot_r[col0 // 512]
                S.op("DVE", lambda e: e.tensor_tensor(out=t1, in0=x1, in1=cosv, op=ALU.mult), reads=rd, writes=[thA_r])
                S.op("DVE", lambda e: e.tensor_tensor(out=t2, in0=x2, in1=sinv, op=ALU.mult), reads=rd, writes=[thB_r])
                S.op("DVE", lambda e: e.tensor_tensor(out=o1, in0=t1, in1=t2, op=ALU.subtract), reads=[thA_r, thB_r], writes=[oh[0]])
                S.op("DVE", lambda e: e.tensor_tensor(out=t1, in0=x1, in1=sinv, op=ALU.mult), reads=rd, writes=[thA_r])
                S.op("DVE", lambda e: e.tensor_tensor(out=t2, in0=x2, in1=cosv, op=ALU.mult), reads=rd, writes=[thB_r])
                S.op("DVE", lambda e: e.tensor_tensor(out=o2, in0=t1, in1=t2, op=ALU.add), reads=[thA_r, thB_r], writes=[oh[1]])

            prq = proj(2056, 512)
            rotary(prq, 0)
            yield
            prk = proj(2568, 512)
            rotary(prk, 512)
            for h in range(4):
                S.op("DVE", lambda e, h=h: e.tensor_scalar(out=kd[:, h * 128:(h + 1) * 128], in0=rot[:, 512 + h * 128:512 + (h + 1) * 128],
                                                           scalar1=kdec[:, h:h + 1], scalar2=None, op0=ALU.mult),
                     reads=rot_r[1] + [cr["kdec"]], writes=[kd_r[h]])
            yield
            prv = proj(3080, 512)
            copy_from_psum(rv[:, :], pj[prv][:, :], [pj_r[prv]], [rv_r])
            prz = proj(3592, 512)
            S.op("ACT", lambda e: e.activation(out=th[:, :], in_=pj[prz][:, :], func=AF.Tanh, scale=0.5), reads=[pj_r[prz]], writes=th_r)
            S.op("DVE", lambda e: e.scalar_tensor_tensor(out=gret[:, :], in0=th[:, :], scalar=1.0, in1=pj[prz][:, :],
                                                         op0=ALU.add, op1=ALU.mult), reads=th_r + [pj_r[prz]], writes=[gret_r])
            yield
            pt2 = nxt("ptr", 2)
            transposes(rot, rot_r, pt2)
            S.op("DVE", lambda e: e.tensor_copy(out=rqT[:, :, :], in_=ptr[pt2][:, 0:4, :]), reads=[ptr_r[pt2]], writes=[rqT_r])
            S.op("DVE", lambda e: e.tensor_tensor(out=rqdT[:, :, :], in0=ptr[pt2][:, 0:4, :], in1=qdec[:, :, :], op=ALU.mult),
                 reads=[ptr_r[pt2], cr["qdec"]], writes=[rqdT_r])
            S.op("DVE", lambda e: e.tensor_copy(out=rkT[:, :, :], in_=ptr[pt2][:, 4:8, :]), reads=[ptr_r[pt2]], writes=[rkT_r])
            yield
            p = proj(2048, 8)
            ecol = 32 + 8 * (i % 2)
            S.op("DVE", lambda e, p=p: e.tensor_tensor(out=smB[:, ecol:ecol + 8], in0=pj[p][:, 0:8], in1=fbt[:, :], op=ALU.add),
                 reads=[pj_r[p], cr["fbt"]], writes=[lfe_r[i % 2]])
            pq = proj(0, 512)
            copy_from_psum(qk[:, 0:512], pj[pq][:, :], [pj_r[pq]], qk_r[0])
            S.op("ACT", lambda e: e.activation(out=smB[:, ecol:ecol + 8], in_=smB[:, ecol:ecol + 8], func=AF.Exp, scale=-1.0),
                 reads=[lfe_r[i % 2]], writes=[lfe_r[i % 2]])
            if i % 2 == 1:
                S.op("ACT", lambda e: e.activation(out=lf[:, :], in_=smB[:, 32:48], func=AF.Ln, bias=1.0),
                     reads=lfe_r, writes=[lf_r])
            pk = proj(512, 512)
            S.op("DVE", lambda e: e.tensor_copy(out=qk[:, 512:1024], in_=pj[pk][:, :]), reads=[pj_r[pk]], writes=qk_r[1])
            yield
            if i == 0:
                S.op("POOL", lambda e: e.memset(st_f[:, :, :].rearrange("p a b -> p (a b)"), 0.0), writes=st_f_r)
                S.op("POOL", lambda e: e.memset(st_b[:, :, :].rearrange("p a b -> p (a b)"), 0.0), writes=st_b_r)

            def ret_pair(hp):
                hs = (2 * hp, 2 * hp + 1)
                RBs = {}
                for h in hs:
                    rb = nxt("pj", 2)
                    RBs[h] = (pj[rb], pj_r[rb])
                sds = {}
                for h in hs:
                    RB, RB_r = RBs[h]
                    S.op("PE", lambda e, h=h, RB=RB: e.matmul(RB[:, 0:128], lhsT=rkT[:, h, :], rhs=rqT[:, h, :], start=True, stop=True),
                         reads=[rkT_r, rqT_r], writes=[RB_r])
                    S.op("PE", lambda e, h=h, RB=RB: e.matmul(RB[:, 256:384], lhsT=kd[:, h * 128:(h + 1) * 128], rhs=rv[:, h * 128:(h + 1) * 128],
                                                              start=True, stop=True), reads=[kd_r[h], rv_r], writes=[RB_r])
                for h in hs:
                    RB, RB_r = RBs[h]
                    sd = nxt("Sd", 2)
                    sds[h] = sd
                    S.op("DVE", lambda e, h=h, RB=RB, sd=sd: e.tensor_tensor(out=Sd[:, sd, :], in0=RB[:, 0:128], in1=dT[:, h, :], op=ALU.mult),
                         reads=[RB_r, cr["dT"]], writes=[Sd_r[sd]])
                    S.op("DVE", lambda e, h=h, RB=RB: e.scalar_tensor_tensor(out=st_f[:, h, :], in0=st_f[:, h, :], scalar=g128[h], in1=RB[:, 256:384],
                                                                             op0=ALU.mult, op1=ALU.add),
                         reads=[st_f_r[h], RB_r], writes=[st_f_r[h]])
                for h in hs:
                    RB, RB_r = RBs[h]
                    sd = sds[h]
                    S.op("PE", lambda e, h=h, RB=RB, sd=sd: e.matmul(RB[:, 128:256], lhsT=Sd[:, sd, :], rhs=rv[:, h * 128:(h + 1) * 128],
                                                                     start=True, stop=False),
                         reads=[Sd_r[sd], rv_r], writes=[RB_r])
                    S.op("PE", lambda e, h=h, RB=RB: e.matmul(RB[:, 128:256], lhsT=rqdT[:, h, :], rhs=st_b[:, h, :], start=False, stop=True),
                         reads=[rqdT_r, st_b_r[h]], writes=[RB_r])
                for h in hs:
                    RB, RB_r = RBs[h]
                    S.op("DVE", lambda e, h=h, RB=RB: e.bn_stats(out=bn[:, h, 0:6], in_=RB[:, 128:256]), reads=[RB_r], writes=[bn_r[h]])
                    S.op("DVE", lambda e, h=h: e.bn_aggr(out=bn[:, h, 6:8], in_=bn[:, h, 0:6]), reads=[bn_r[h]], writes=[bn_r[h]])
                    S.op("DVE", lambda e, h=h: e.tensor_scalar(out=smB[:, 24 + h:25 + h], in0=bn[:, h, 7:8], scalar1=GN_EPS, scalar2=None,
                                                               op0=ALU.add), reads=[bn_r[h]], writes=[gn_r[h]])
                for h in hs:
                    S.op("POOL", lambda e, h=h: e.tensor_tensor(out=smB[:, 16 + h:17 + h], in0=smB[:, 24 + h:25 + h],
                                                                in1=negh[:, :], op=ALU.pow), reads=[gn_r[h], cr["negh"]], writes=[gn_r[h]])
                    S.op("POOL", lambda e, h=h: e.tensor_copy(out=st_b[:, h, :], in_=st_f[:, h, :]), reads=[st_f_r[h]], writes=[st_b_r[h]])
                for h in hs:
                    S.op("DVE", lambda e, h=h: e.scalar_tensor_tensor(out=smB[:, 20 + h:21 + h], in0=bn[:, h, 6:7], scalar=-1.0,
                                                                      in1=smB[:, 16 + h:17 + h], op0=ALU.mult, op1=ALU.mult),
                         reads=[bn_r[h], gn_r[h]], writes=[gn_r[h]])
                for h in hs:
                    RB, RB_r = RBs[h]
                    if evac == "ACT":
                        S.op("ACT", lambda e, h=h, RB=RB: e.activation(out=RB[:, 128:256], in_=RB[:, 128:256], func=AF.Identity,
                                                                       scale=smB[:, 16 + h:17 + h], bias=smB[:, 20 + h:21 + h]),
                             reads=[RB_r, gn_r[h]], writes=[RB_r])
                    else:
                        S.op("DVE", lambda e, h=h, RB=RB: e.tensor_scalar(out=RB[:, 128:256], in0=RB[:, 128:256], scalar1=smB[:, 16 + h:17 + h],
                                                                          scalar2=smB[:, 20 + h:21 + h], op0=ALU.mult, op1=ALU.add),
                             reads=[RB_r, gn_r[h]], writes=[RB_r])
                for h in hs:
                    RB, RB_r = RBs[h]
                    S.op("DVE", lambda e, h=h, RB=RB: e.tensor_tensor(out=mixed[g % 4][:, 512 + h * 128:512 + (h + 1) * 128], in0=RB[:, 128:256],
                                                                      in1=gret[:, h * 128:(h + 1) * 128], op=ALU.mult),
                         reads=[RB_r, gret_r], writes=[mixr_r[g % 4][h]])

            ret_pair(0)
            yield
            cA, cM = carry_r[0], carry_r[1]
            if i == 0:
                S.op("DVE", lambda e: e.memset(carry[:, 0, :], 0.0), writes=[cA])
            if i % 2 == 1:
                pc_ = nxt("pj", 2)
                for t_ in range(2):
                    S.op("PE", lambda e, t_=t_: e.matmul(pj[pc_][:, 16 * t_:16 * t_ + 8], lhsT=triu[:, :], rhs=lf[:, 8 * t_:8 * t_ + 8],
                                                         start=True, stop=True), reads=[lf_r, cr["triu"]], writes=[pj_r[pc_]])
                    S.op("PE", lambda e, t_=t_: e.matmul(pj[pc_][:, 16 * t_ + 8:16 * t_ + 16], lhsT=onesf[:, :], rhs=lf[:, 8 * t_:8 * t_ + 8],
                                                         start=True, stop=True), reads=[lf_r, cr["ones"]], writes=[pj_r[pc_]])
                S.op("DVE", lambda e: e.tensor_tensor(out=a_all[:, i - 1, :], in0=pj[pc_][:, 0:8], in1=carry[:, 0, :], op=ALU.add),
                     reads=[pj_r[pc_], cA], writes=[aall_r])
                S.op("DVE", lambda e: e.tensor_tensor(out=carry[:, 1, :], in0=pj[pc_][:, 8:16], in1=carry[:, 0, :], op=ALU.add),
                     reads=[pj_r[pc_], cA], writes=[cM])
                S.op("DVE", lambda e: e.tensor_tensor(out=a_all[:, i, :], in0=pj[pc_][:, 16:24], in1=carry[:, 1, :], op=ALU.add),
                     reads=[pj_r[pc_], cM], writes=[aall_r])
                S.op("DVE", lambda e: e.tensor_tensor(out=carry[:, 0, :], in0=pj[pc_][:, 24:32], in1=carry[:, 1, :], op=ALU.add),
                     reads=[pj_r[pc_], cM], writes=[cA])
            pt = nxt("ptr", 2)
            transposes(qk, qk_r, pt)
            for e_ in range(2):
                S.op("DVE", lambda e, e_=e_: e.tensor_copy(out=QT[pm][e_ * 64:(e_ + 1) * 64, :, e_, (g % 2) * 128:(g % 2 + 1) * 128],
                                                           in_=ptr[pt][e_ * 64:(e_ + 1) * 64, 0:4, :]),
                     reads=[ptr_r[pt]], writes=[QT_r[pm][g % 2][e_]])
            S.op("DVE", lambda e: e.tensor_copy(out=KT[:, :, i * 128:(i + 1) * 128], in_=ptr[pt][:, 4:8, :]),
                 reads=[ptr_r[pt]], writes=[KT_r[i]])
            yield
            ret_pair(1)
            yield
            if i % 2 == 1:
                S.op("DVE", lambda e: e.tensor_scalar(out=smB[:, 8:16], in0=carry[:, 1, :], scalar1=-1.0, scalar2=None, op0=ALU.mult),
                     reads=[cM], writes=[aref_r])
                for h in range(FOX_HEADS):
                    S.op("DVE", lambda e, h=h: e.tensor_scalar(out=biasm[pm][:, 0:i + 1, h], in0=a_all[:, 0:i + 1, h],
                                                               scalar1=smB[:, 8 + h:9 + h], scalar2=None, op0=ALU.add),
                         reads=[aref_r, aall_r], writes=[biasm_r[pm][h]])
            pv = proj(1024, 512)
            copy_from_psum(VA[:, i, :, 0:64], pj[pv][:, :].rearrange("p (h d) -> p h d", h=FOX_HEADS), [pj_r[pv]], [VA_r[i]])
            yield
            pz = proj(1536, 512)
            S.op("ACT", lambda e: e.activation(out=th[:, :], in_=pj[pz][:, :], func=AF.Tanh, scale=0.5), reads=[pj_r[pz]], writes=th_r)
            S.op("DVE", lambda e: e.scalar_tensor_tensor(out=gfox[g % 4][:, :], in0=th[:, :], scalar=1.0, in1=pj[pz][:, :],
                                                         op0=ALU.add, op1=ALU.mult), reads=th_r + [pj_r[pz]], writes=[gfox_r[g % 4]])

        def stage_C(mg):
            g0 = 2 * mg
            i0 = g0 % NT
            i1 = i0 + 1
            pm = mg % 2
            blocks = [(h, j) for h in range(FOX_HEADS) for j in range(i1 + 1)]
            groups = [blocks[k:k + 2] for k in range(0, len(blocks), 2)]
            info = {}

            def emit_S(gi):
                sb_ = nxt("pS", 2)
                bank = pS[sb_][:, :, :].rearrange("p a b -> p (a b)")
                for q, (h, j) in enumerate(groups[gi]):
                    p_, e_ = h // 2, h % 2
                    dst = bank[:, q * 256:(q + 1) * 256]
                    lo = 128 if j == i1 else 0
                    diag = (j >= i0)
                    S.op("PE", lambda e, dst=dst, p_=p_, e_=e_, j=j, lo=lo, diag=diag: e.matmul(
                        dst[:, lo:256], lhsT=KT[:, p_, j * 128:(j + 1) * 128],
                        rhs=QT[pm][:, p_, e_, lo:256], start=True, stop=not diag),
                         reads=[KT_r[j], QT_r[pm][0][e_], QT_r[pm][1][e_]], writes=[pS_r[sb_]])
                    if diag:
                        S.op("PE", lambda e, dst=dst, lo=lo: e.matmul(dst[:, lo:lo + 128], lhsT=ident[:, :], rhs=maskneg[:, :],
                                                                     start=False, stop=True),
                             reads=[cr["ident"], cr["mask"]], writes=[pS_r[sb_]])
                for q, (h, j) in enumerate(groups[gi]):
                    dst = bank[:, q * 256:(q + 1) * 256]
                    lo = 128 if j == i1 else 0
                    pt_ = nxt("PT", NPT)
                    S.op("ACT", lambda e, dst=dst, pt_=pt_, h=h, j=j, lo=lo: e.activation(out=PT[:, pt_, lo:256], in_=dst[:, lo:256], func=AF.Exp,
                                                                                        scale=0.125, bias=biasm[pm][:, j, h:h + 1]),
                         reads=[pS_r[sb_], biasm_r[pm][h]], writes=[PT_r[pt_]])
                    info[(h, j)] = pt_

            def emit_PV(gi):
                for (h, j) in groups[gi]:
                    pt_ = info[(h, j)]
                    for q in range(2):
                        iq = i0 + q
                        if j > iq:
                            continue
                        ob = h % 2
                        S.op("PE", lambda e, pt_=pt_, h=h, j=j, q=q, iq=iq, ob=ob: e.matmul(
                            pO[ob][:, q * 128:q * 128 + 65], lhsT=PT[:, pt_, q * 128:(q + 1) * 128], rhs=VA[:, j, h, :],
                            start=(j == 0 and q == 0), stop=(j == iq), skip_group_check=True),
                             reads=[PT_r[pt_], VA_r[j]], writes=[pO_r[ob]])
                        if j == i1 and q == 1:
                            for qq in range(2):
                                mb = (g0 + qq) % 4
                                S.op("DVE", lambda e, h=h, qq=qq, ob=ob: e.reciprocal(out=smC[:, qq, h:h + 1],
                                                                                     in_=pO[ob][:, qq * 128 + 64:qq * 128 + 65]),
                                     reads=[pO_r[ob]], writes=[rl_r[qq]])
                            for qq in range(2):
                                mb = (g0 + qq) % 4
                                S.op("DVE", lambda e, h=h, qq=qq, mb=mb, ob=ob: e.scalar_tensor_tensor(
                                    out=mixed[mb][:, h * 64:(h + 1) * 64], in0=pO[ob][:, qq * 128:qq * 128 + 64],
                                    scalar=smC[:, qq, h:h + 1], in1=gfox[mb][:, h * 64:(h + 1) * 64], op0=ALU.mult, op1=ALU.mult),
                                     reads=[pO_r[ob], rl_r[qq], gfox_r[mb]], writes=[mixf_r[mb]])

            ng = len(groups)
            for gi in range(ng + 1):
                if gi < ng:
                    emit_S(gi)
                if gi - 1 >= 0:
                    emit_PV(gi - 1)
                yield

        def stage_D(g):
            pb = g % 4
            X, X_r = big[1], big_r[1]
            S.dma(lambda e: e.dma_start(out=X[:, :], in_=x_d[g * 128:(g + 1) * 128, :]), writes=[X_r])
            pt = nxt("ptr", 2)
            for c in range(8):
                S.op("PE", lambda e, c=c: e.transpose(out=ptr[pt][:, c, :], in_=mixed[pb][:, c * 128:(c + 1) * 128], identity=ident[:, :]),
                     reads=[mixf_r[pb]] + mixr_r[pb] + [cr["ident"]], writes=[ptr_r[pt]])
                tick()
            S.op("DVE", lambda e: e.tensor_scalar(out=mixT[:, :, :], in0=ptr[pt][:, :, :], scalar1=0.5, scalar2=None, op0=ALU.mult),
                 reads=[ptr_r[pt]], writes=[mixT_r])
            yield
            ph = []
            for half in range(2):
                p = nxt("pj", 2)
                for kc in range(8):
                    S.op("PE", lambda e, kc=kc, p=p, half=half: e.matmul(pj[p][:, :], lhsT=mixT[:, kc, :],
                                                                         rhs=WO[:, kc, half * 512:(half + 1) * 512],
                                                                         start=(kc == 0), stop=(kc == 7)),
                         reads=[mixT_r, WO_r[kc]], writes=[pj_r[p]])
                    tick()
                ph.append(p)
                S.op("ACT", lambda e, p=p, half=half: e.activation(out=junk[:, :], in_=pj[p][:, :], func=AF.Square,
                                                                   accum_out=smD[:, half:half + 1]),
                     reads=[pj_r[p]], writes=[junk_r[half], ssD_r[half]])
            S.op("DVE", lambda e: e.tensor_tensor(out=smD[:, 2:3], in0=smD[:, 0:1], in1=smD[:, 1:2], op=ALU.add),
                 reads=ssD_r, writes=[msD_r])
            S.op("DVE", lambda e: e.tensor_scalar(out=smD[:, 2:3], in0=smD[:, 2:3], scalar1=1.0 / D_MODEL,
                                                  scalar2=NORM_EPS, op0=ALU.mult, op1=ALU.add), reads=[msD_r], writes=[msD_r])
            S.op("POOL", lambda e: e.tensor_tensor(out=smD[:, 3:4], in0=smD[:, 2:3], in1=negh[:, :], op=ALU.pow),
                 reads=[msD_r, cr["negh"]], writes=[rstdD_r])
            for half in range(2):
                p = ph[half]
                S.op("DVE", lambda e, p=p, half=half: e.scalar_tensor_tensor(out=pj[p][:, :], in0=pj[p][:, :], scalar=smD[:, 3:4],
                                                                             in1=g2t[:, half * 512:(half + 1) * 512],
                                                                             op0=ALU.mult, op1=ALU.mult),
                     reads=[pj_r[p], rstdD_r, cr["g2t"]], writes=[pj_r[p]])
                S.op("DVE", lambda e, p=p, half=half: e.tensor_tensor(out=X[:, half * 512:(half + 1) * 512], in0=pj[p][:, :],
                                                                      in1=X[:, half * 512:(half + 1) * 512], op=ALU.add),
                     reads=[pj_r[p], X_r], writes=[X_r])
            if fill["fn"] is not None:
                fill["fn"](8)
            yield
            S.dma(lambda e: e.dma_start(out=y_d[g * 128:(g + 1) * 128, :], in_=X[:, :]), reads=[X_r])

        NG = NSEQ * NT
        assert NT % 2 == 0
        catt = None
        d_queue = []
        it = 0
        next_pair = 0
        ROUNDS = 12

        def c_steps(n):
            nonlocal catt
            for _ in range(n):
                if catt is None:
                    return
                try:
                    next(catt[1])
                    catt[4] += 1
                except StopIteration:
                    d_queue.extend([2 * catt[0], 2 * catt[0] + 1])
                    catt = None

        while True:
            gens = []
            if 0 <= it - 1 < NG:
                gens.append(stage_B(it - 1))
            if it < NG:
                gens.append(stage_A(it))
            if d_queue:
                gens.append(stage_D(d_queue.pop(0)))
            if catt is None and next_pair * 2 + 1 <= it - 2 and next_pair * 2 < NG:
                i1 = (2 * next_pair) % NT + 1
                catt = [next_pair, stage_C(next_pair), it, 4 * (i1 + 1) + 1, 0]
                next_pair += 1
            if it - 1 > 0 and (it - 1) % NT == 0 and it - 1 < NG + 1:
                c_steps(10 ** 6)
            k = 1
            fill["fn"] = c_steps
            fill["rate"] = 0.0
            fill["credit"] = 0.0
            if catt is not None:
                remaining = catt[3] - catt[4]
                target = remaining if (it - catt[2] >= 1 or it >= NG) else (catt[3] + 1) // 2
                fill["rate"] = max(0.0, (target - ROUNDS)) / 130.0
                k = 1
            while gens:
                first = True
                for gen in list(gens):
                    try:
                        next(gen)
                    except StopIteration:
                        gens.remove(gen)
                    if first:
                        c_steps(k)
                        first = False
            if catt is not None and (it - catt[2] >= 1 or it >= NG):
                c_steps(10 ** 6)
            it += 1
            if it > NG + 1 and catt is None and not d_queue and next_pair * 2 >= NG:
                break

        sem_names = ["PE", "ACT", "DVE", "POOL"]
        sems = {n: es.enter_context(nc.semaphore(f"s_{n}")) for n in sem_names}
        dsems = [es.enter_context(nc.semaphore(f"d_{k}")) for k in range(NDS)]
        qsems = [es.enter_context(nc.semaphore(f"q_{k}")) for k in range(S.n_qdma)]
        block = es.enter_context(nc.Block())
        S.emit(nc, block, sems, dsems, qsems)
    return nc


def kernel(x, pre_norm_gain, w_in, fox_forget_bias, w_out, post_norm_gain):
    x = np.asarray(x, np.float32)
    B, SEQ, D = x.shape
    NSEQ = B // N_CORES
    NT = SEQ // 128
    consts, g128 = make_consts(SEQ)
    nc = build_nc(NSEQ, NT, g128)
    shared = dict(
        w_in=np.ascontiguousarray(np.asarray(w_in, np.float32)[0]),
        w_out=np.ascontiguousarray(np.asarray(w_out, np.float32)[0]),
        g1t=np.ascontiguousarray(np.asarray(pre_norm_gain, np.float32)[0].reshape(8, 128).T),
        g2=np.ascontiguousarray(np.asarray(post_norm_gain, np.float32)[0].reshape(1, D)),
        fb=np.ascontiguousarray(np.asarray(fox_forget_bias, np.float32)[0].reshape(1, FOX_HEADS)),
        **consts,
    )
    in_maps = []
    for c in range(N_CORES):
        m = dict(shared)
        m["x"] = np.ascontiguousarray(x[c * NSEQ:(c + 1) * NSEQ].reshape(NSEQ * SEQ, D))
        in_maps.append(m)
    res = run_bass_kernel_spmd(nc, in_maps, core_ids=list(range(N_CORES)))
    out = np.concatenate([np.asarray(r["y"], np.float32).reshape(NSEQ, SEQ, D) for r in res.results], axis=0)
    return out
```

```python
import math
import os
from contextlib import ExitStack

import numpy as np
import ml_dtypes
import concourse.bass as bass
import concourse.mybir as mybir
from concourse.bass_utils import run_bass_kernel_spmd

F32 = mybir.dt.float32
BF16 = mybir.dt.bfloat16
ALU = mybir.AluOpType
AF = mybir.ActivationFunctionType

D_MODEL = 1024
D_IN = 4104
FOX_HEADS = 8
RET_HEADS = 4
NORM_EPS = 1e-6
GN_EPS = 1e-5
N_CORES = 8

SAME_ENGINE_SYNC = bool(int(os.environ.get("KSES", "1")))

STAGE = float(os.environ.get('KSTAGE', '99'))
NDS = 16
DUMP = bool(int(os.environ.get('KDUMP', '0')))


class Res:
    __slots__ = ("name", "last_w", "readers")

    def __init__(self, name):
        self.name = name
        self.last_w = None
        self.readers = []


class Op:
    __slots__ = ("fn", "deps", "dma_k", "needed")

    def __init__(self, fn, deps, dma_k=None):
        self.fn = fn
        self.deps = deps
        self.dma_k = dma_k
        self.needed = False


class Sched:
    ENGS = ("SP", "PE", "ACT", "DVE", "POOL")

    def __init__(self):
        self.ops = {e: [] for e in self.ENGS}
        self.n_dma = 0
        self.n_qdma = 0

    def op(self, eng, fn, reads=(), writes=()):
        deps = set()
        for r in reads:
            if r.last_w is not None:
                deps.add(r.last_w)
        for w in writes:
            if w.last_w is not None:
                deps.add(w.last_w)
            for rd in w.readers:
                deps.add(rd)
        idx = len(self.ops[eng])
        me = (eng, idx)
        deps.discard(me)
        self.ops[eng].append(Op(fn, deps))
        for r in reads:
            r.readers.append(me)
        for w in writes:
            w.last_w = me
            w.readers = []
        return me

    def dma(self, fn, reads=(), writes=(), eng="SP"):
        me = self.op(eng, fn, reads, writes)
        if eng == "SP":
            self.ops[eng][me[1]].dma_k = self.n_dma
            self.n_dma += 1
        else:
            self.ops[eng][me[1]].dma_k = -1 - self.n_qdma
            self.n_qdma += 1
        return me

    def emit(self, nc, block, sems, dsems, qsems):
        for e in self.ENGS:
            for o in self.ops[e]:
                for (e2, i2) in o.deps:
                    if self.ops[e2][i2].dma_k is not None:
                        continue
                    if e2 == e and (e == "PE" or not SAME_ENGINE_SYNC):
                        continue
                    self.ops[e2][i2].needed = True
        cnt = {}
        for e in self.ENGS:
            if e == "SP":
                continue
            c = 0
            arr = []
            for o in self.ops[e]:
                if o.needed:
                    c += 1
                arr.append(c)
            cnt[e] = arr

        def run(e, engobj):
            waited = {}
            for idx, o in enumerate(self.ops[e]):
                need = {}
                for (e2, i2) in o.deps:
                    if self.ops[e2][i2].dma_k is not None:
                        k = self.ops[e2][i2].dma_k
                        if k < 0:
                            key = ("Q", -1 - k)
                            val = 16
                        else:
                            key = ("D", k % NDS)
                            val = 16 * (k // NDS + 1)
                    else:
                        if e2 == e and (e == "PE" or not SAME_ENGINE_SYNC):
                            continue
                        key = ("C", e2)
                        val = cnt[e2][i2]
                    if val > need.get(key, 0):
                        need[key] = val
                if o.dma_k is not None and o.dma_k >= NDS:
                    key = ("D", o.dma_k % NDS)
                    val = 16 * (o.dma_k // NDS)
                    if val > need.get(key, 0):
                        need[key] = val
                for key, val in need.items():
                    if waited.get(key, 0) >= val:
                        continue
                    waited[key] = val
                    s = dsems[key[1]] if key[0] == "D" else (qsems[key[1]] if key[0] == "Q" else sems[key[1]])
                    engobj.wait_ge(s, val)
                    if DUMP:
                        print("   ", e, idx, "WAIT", key, val)
                ins = o.fn(engobj)
                if DUMP:
                    print(e, idx, "deps", sorted(o.deps), "inc" if o.needed else "", o.dma_k)
                if o.dma_k is not None and o.dma_k < 0:
                    ins.then_inc(qsems[-1 - o.dma_k], 16)
                elif o.dma_k is not None:
                    ins.then_inc(dsems[o.dma_k % NDS], 16)
                elif o.needed:
                    ins.then_inc(sems[e], 1)
            if e == "SP":
                for s in range(min(NDS, self.n_dma)):
                    last_k = ((self.n_dma - 1 - s) // NDS) * NDS + s
                    engobj.wait_ge(dsems[s], 16 * (last_k // NDS + 1))

        @block.sync
        def _(e):
            run("SP", e)

        @block.tensor
        def _(e):
            run("PE", e)

        @block.scalar
        def _(e):
            run("ACT", e)

        @block.vector
        def _(e):
            run("DVE", e)

        @block.gpsimd
        def _(e):
            run("POOL", e)


def bcast_ap(ap, dims):
    return bass.AP(tensor=ap.tensor, offset=ap.offset, ap=[list(d) for d in dims])


def make_consts(seq_len):
    lg = np.log1p(-np.exp(np.linspace(math.log(1.0 / 32), math.log(1.0 / 512), RET_HEADS))).astype(np.float32)
    lg64 = lg.astype(np.float64)
    ident = np.eye(128, dtype=np.float32).astype(ml_dtypes.bfloat16)
    s_idx = np.arange(128)[:, None]
    t_idx = np.arange(128)[None, :]
    maskneg = np.where(s_idx <= t_idx, 0.0, -30000.0).astype(np.float32).astype(ml_dtypes.bfloat16)
    triu = (s_idx <= t_idx).astype(np.float32)
    j = np.arange(128)[:, None]
    i = np.arange(128)[None, :]
    same = (i // 64) == (j // 64)
    fwd = (j // 64 == 0) & (i // 64 == 1)
    dT = np.zeros((128, RET_HEADS, 128), np.float32)
    for h in range(RET_HEADS):
        d = np.where(same, np.exp(lg64[h] * np.abs(i - j)), np.where(fwd, np.exp(lg64[h] * (i - j)), 0.0))
        dT[:, h, :] = (d * (128.0 ** -0.5)).astype(np.float32)
    il = np.arange(128, dtype=np.float64)
    qdec = np.stack([np.exp(lg64[h] * (il + 1.0)) for h in range(RET_HEADS)], 0).astype(np.float32)
    kdec = np.stack([np.exp(lg64[h] * (127.0 - il)) * (128.0 ** -0.5) for h in range(RET_HEADS)], 1).astype(np.float32)
    g128 = [float(np.exp(lg64[h] * 128.0)) for h in range(RET_HEADS)]
    half = 64
    inv = (1.0 / (10000.0 ** (np.arange(half, dtype=np.float32) / half))).astype(np.float32)
    ang = np.arange(seq_len, dtype=np.float32)[:, None] * inv[None, :]
    cossin = np.concatenate([np.cos(ang), np.sin(ang)], axis=1).astype(np.float32)
    return dict(ident=ident, maskneg=maskneg, triu=triu, dT=dT.reshape(128, RET_HEADS * 128),
                qdec=qdec.reshape(1, RET_HEADS * 128), kdec=kdec, cossin=cossin), g128


def build_nc(NSEQ, NT, g128):
    nc = bass.Bass("TRN2", target_bir_lowering=False, dynamic_dma_scratch_size=4096)
    NTOK = NSEQ * NT * 128
    SEQL = NT * 128
    x_d = nc.dram_tensor("x", [NTOK, D_MODEL], F32, kind="ExternalInput").ap()
    win_d = nc.dram_tensor("w_in", [D_MODEL, D_IN], F32, kind="ExternalInput").ap()
    wout_d = nc.dram_tensor("w_out", [D_MODEL, D_MODEL], F32, kind="ExternalInput").ap()
    g1t_d = nc.dram_tensor("g1t", [128, 8], F32, kind="ExternalInput").ap()
    g2_d = nc.dram_tensor("g2", [1, D_MODEL], F32, kind="ExternalInput").ap()
    fb_d = nc.dram_tensor("fb", [1, FOX_HEADS], F32, kind="ExternalInput").ap()
    ident_d = nc.dram_tensor("ident", [128, 128], BF16, kind="ExternalInput").ap()
    mask_d = nc.dram_tensor("maskneg", [128, 128], BF16, kind="ExternalInput").ap()
    triu_d = nc.dram_tensor("triu", [128, 128], F32, kind="ExternalInput").ap()
    dT_d = nc.dram_tensor("dT", [128, 512], F32, kind="ExternalInput").ap()
    qdec_d = nc.dram_tensor("qdec", [1, 512], F32, kind="ExternalInput").ap()
    kdec_d = nc.dram_tensor("kdec", [128, 4], F32, kind="ExternalInput").ap()
    cs_d = nc.dram_tensor("cossin", [SEQL, 128], F32, kind="ExternalInput").ap()
    y_d = nc.dram_tensor("y", [NTOK, D_MODEL], F32, kind="ExternalOutput").ap()

    S = Sched()
    es = ExitStack()

    def sb(name, shape, dt):
        return es.enter_context(nc.sbuf_tensor(name, shape, dt))

    def ps(name, shape, dt):
        return es.enter_context(nc.psum_tensor(name, shape, dt))

    with es:
        W = sb("W", [128, 8, D_IN], BF16)
        WO = sb("WO", [128, 8, D_MODEL], BF16)
        KT = sb("KT", [128, 4, SEQL], BF16)
        VA = sb("VA", [128, NT, FOX_HEADS, 65], BF16)
        KT_r = [Res(f"KT{t}") for t in range(NT)]
        VA_r = [Res(f"VA{t}") for t in range(NT)]
        a_all = sb("a_all", [128, NT, FOX_HEADS], F32)
        aall_r = Res("a_all")
        g2t = sb("g2t", [128, D_MODEL], F32)
        fbt = sb("fbt", [128, FOX_HEADS], F32)
        g1t = sb("g1t_s", [128, 8], F32)
        NBIG = 2
        big = [sb(f"big{i}", [128, 1024], F32) for i in range(NBIG)]
        big_r = [Res(f"big{i}") for i in range(NBIG)]
        xn = sb("xn", [128, 1024], BF16)
        xn_r = Res("xn")
        junk = sb("junk", [128, 512], BF16)
        _jr = Res("junk")
        junk_r = [_jr, _jr]
        uT = [sb(f"uT{k}", [128, 8, 128], BF16) for k in range(2)]
        uT_r = [[Res(f"uT{k}_{c}") for c in range(8)] for k in range(2)]
        qk = sb("qk", [128, 1024], BF16)
        qk_r = [[Res(f"qk{a}_{b}") for b in range(2)] for a in range(2)]
        QT = [sb(f"QT{k}", [128, 4, 2, 256], BF16) for k in range(2)]
        QT_r = [[[Res(f"QT{k}_{q}_{e}") for e in range(2)] for q in range(2)] for k in range(2)]
        gfox = [sb(f"gfox{k}", [128, 512], F32) for k in range(4)]
        gfox_r = [Res(f"gfox{k}") for k in range(4)]
        gret = sb("gret", [128, 512], F32)
        gret_r = Res("gret")
        th = sb("th", [128, 512], F32)
        thA_r, thB_r = Res("thA"), Res("thB")
        th_r = [thA_r, thB_r]
        rot, rot_r = qk, qk_r
        kd = sb("kd", [128, 512], BF16)
        kd_r = [Res(f"kd{h}") for h in range(4)]
        rqT = sb("rqT", [128, 4, 128], BF16)
        rkT = sb("rkT", [128, 4, 128], BF16)
        rqdT = sb("rqdT", [128, 4, 128], BF16)
        rqT_r, rkT_r, rqdT_r = Res("rqT"), Res("rkT"), Res("rqdT")
        rv = sb("rv", [128, 512], BF16)
        rv_r = Res("rv")
        st_f = sb("st_f", [128, 4, 128], F32)
        st_b = sb("st_b", [128, 4, 128], BF16)
        st_f_r = [Res(f"stf{h}") for h in range(4)]
        st_b_r = [Res(f"stb{h}") for h in range(4)]
        NPT = 6
        PT = sb("PT", [128, NPT, 256], BF16)
        PT_r = [Res(f"PT{i}") for i in range(NPT)]
        Sd = sb("Sd", [128, 2, 128], BF16)
        Sd_r = [Res("Sd0"), Res("Sd1")]
        mixed = [sb(f"mixed{k}", [128, 1024], BF16) for k in range(4)]
        mixf_r = [Res(f"mixf{k}") for k in range(4)]
        mixr_r = [[Res(f"mixr{k}_{h}") for h in range(4)] for k in range(4)]
        mixT = sb("mixT", [128, 8, 128], BF16)
        mixT_r = Res("mixT")
        ident = sb("ident_s", [128, 128], BF16)
        maskneg = sb("mask_s", [128, 128], BF16)
        triu = sb("triu_s", [128, 128], F32)
        onesf = sb("onesf", [128, 128], F32)
        dT = sb("dT_s", [128, 4, 128], F32)
        qdec = sb("qdec_s", [128, 4, 128], F32)
        kdec = sb("kdec_s", [128, 4], F32)
        cst = sb("cst", [128, 2, 128], F32)
        cst_r = [Res("cs0"), Res("cs1")]
        smA = sb("smA", [128, 8], F32)
        smB = sb("smB", [128, 64], F32)
        smC = sb("smC", [128, 2, 8], F32)
        smD = sb("smD", [128, 8], F32)
        negh = sb("negh", [128, 1], F32)
        bn = sb("bn", [128, 4, 8], F32)
        biasm = [sb(f"biasm{k}", [128, NT, FOX_HEADS], F32) for k in range(2)]
        biasm_r = [[Res(f"biasm{k}_{h}") for h in range(FOX_HEADS)] for k in range(2)]
        carry = sb("carry", [128, 2, FOX_HEADS], F32)
        lf = sb("lf", [128, 2 * FOX_HEADS], F32)
        lfe_r = [Res("lfe0"), Res("lfe1")]
        W_r = [[Res(f"W{kc}_{pc}") for pc in range(4)] for kc in range(8)]
        WO_r = [Res(f"WO{kc}") for kc in range(8)]
        cr = {n: Res("c_" + n) for n in ["ident", "mask", "triu", "dT", "qdec", "kdec", "g1t", "g2t", "fbt", "ones", "negh"]}
        ssA_r, msA_r, rstdA_r = Res("ssA"), Res("msA"), Res("rstdA")
        ssD_r, msD_r, rstdD_r = [Res("ssD0"), Res("ssD1")], Res("msD"), Res("rstdD")
        rl_r = [Res("rl0"), Res("rl1")]
        aref_r = Res("aref")
        gn_r = [Res(f"gn{h}") for h in range(4)]
        bn_r = [Res(f"bn{h}") for h in range(4)]
        carry_r = [Res("carry0"), Res("carry1")]
        lf_r = Res("lf")

        pj = [ps(f"pj{i}", [128, 512], F32) for i in range(2)]
        pj_r = [Res(f"pj{i}") for i in range(2)]
        ptr = [ps(f"ptr{i}", [128, 8, 128], BF16) for i in range(2)]
        ptr_r = [Res(f"ptr{i}") for i in range(2)]
        pS = [ps(f"pS{i}", [128, 4, 128], F32) for i in range(2)]
        pS_r = [Res(f"pS{i}") for i in range(2)]
        pO = [ps(f"pO{i}", [128, 512], F32) for i in range(2)]
        pO_r = [Res(f"pO{i}") for i in range(2)]

        cnt = {"pj": 0, "ptr": 0, "pS": 0, "PT": 0, "Sd": 0}

        def nxt(k, n):
            v = cnt[k] % n
            cnt[k] += 1
            return v

        S.dma(lambda e: e.dma_start(out=ident[:, :], in_=ident_d[:, :]), writes=[cr["ident"]])
        S.dma(lambda e: e.dma_start(out=maskneg[:, :], in_=mask_d[:, :]), writes=[cr["mask"]])
        S.dma(lambda e: e.dma_start(out=triu[:, :], in_=triu_d[:, :]), writes=[cr["triu"]])
        S.dma(lambda e: e.dma_start(out=dT[:, :, :].rearrange("p a b -> p (a b)"), in_=dT_d[:, :]), writes=[cr["dT"]])
        S.dma(lambda e: e.dma_start(out=qdec[:, :, :].rearrange("p a b -> p (a b)"),
                                    in_=bcast_ap(qdec_d, [[0, 128], [1, 512]])), writes=[cr["qdec"]])
        S.dma(lambda e: e.dma_start(out=kdec[:, :], in_=kdec_d[:, :]), writes=[cr["kdec"]])
        S.dma(lambda e: e.dma_start(out=g1t[:, :], in_=g1t_d[:, :]), writes=[cr["g1t"]])
        S.dma(lambda e: e.dma_start(out=g2t[:, :], in_=bcast_ap(g2_d, [[0, 128], [1, D_MODEL]])), writes=[cr["g2t"]])
        S.dma(lambda e: e.dma_start(out=fbt[:, :], in_=bcast_ap(fb_d, [[0, 128], [1, FOX_HEADS]])), writes=[cr["fbt"]])
        S.op("POOL", lambda e: e.memset(onesf[:, :], 1.0), writes=[cr["ones"]])
        S.op("POOL", lambda e: e.memset(negh[:, :], -0.5), writes=[cr["negh"]])
        S.op("POOL", lambda e: e.memset(VA[:, :, :, :].rearrange("p a b c -> p (a b c)"), 1.0), writes=VA_r)
        for k in range(2):
            S.op("POOL", lambda e, k=k: e.memset(QT[k][:, :, :, :].rearrange("p a b c -> p (a b c)"), 0.0),
                 writes=[QT_r[k][q][e_] for q in range(2) for e_ in range(2)])

        nb = 0
        for pc in (2, 3, 0, 1):
            for kc in range(8):
                c0 = pc * 1026
                S.dma(lambda e, kc=kc, c0=c0: e.dma_start(out=W[:, kc, c0:c0 + 1026], in_=win_d[kc * 128:(kc + 1) * 128, c0:c0 + 1026],
                                                          max_dma_last_dim=4104),
                      writes=[W_r[kc][pc]], eng="POOL")
        for kc in range(8):
            S.dma(lambda e, kc=kc: e.dma_start(out=WO[:, kc, :], in_=wout_d[kc * 128:(kc + 1) * 128, :], max_dma_last_dim=4096),
                  writes=[WO_r[kc]], eng="POOL")

        fill = {"rate": 0.0, "credit": 0.0, "fn": None}

        def tick(n=1):
            if fill["fn"] is None or fill["rate"] <= 0.0:
                return
            fill["credit"] += n * fill["rate"]
            if fill["credit"] >= 1.0:
                k_ = int(fill["credit"])
                fill["credit"] -= k_
                fill["fn"](k_)

        def transposes(src, src_r, pt):
            for c in range(8):
                S.op("PE", lambda e, c=c: e.transpose(out=ptr[pt][:, c, :], in_=src[:, c * 128:(c + 1) * 128], identity=ident[:, :]),
                     reads=(src_r[c // 4] if isinstance(src_r, list) else [src_r]) + [cr["ident"]], writes=[ptr_r[pt]])
                tick()

        def stage_A(g):
            i = g % NT
            X, X_r = big[0], big_r[0]
            S.dma(lambda e: e.dma_start(out=X[:, :], in_=x_d[g * 128:(g + 1) * 128, :]), writes=[X_r])
            S.dma(lambda e: e.dma_start(out=cst[:, g % 2, :], in_=cs_d[i * 128:(i + 1) * 128, :]), writes=[cst_r[g % 2]])
            yield
            S.op("ACT", lambda e: e.activation(out=xn[:, :], in_=X[:, :], func=AF.Square, accum_out=smA[:, 0:1]),
                 reads=[X_r], writes=[xn_r, ssA_r])
            yield
            S.op("DVE", lambda e: e.tensor_scalar(out=smA[:, 1:2], in0=smA[:, 0:1], scalar1=1.0 / D_MODEL,
                                                  scalar2=NORM_EPS, op0=ALU.mult, op1=ALU.add), reads=[ssA_r], writes=[msA_r])
            yield
            S.op("POOL", lambda e: e.tensor_tensor(out=smA[:, 2:3], in0=smA[:, 1:2], in1=negh[:, :], op=ALU.pow),
                 reads=[msA_r, cr["negh"]], writes=[rstdA_r])
            yield
            S.op("DVE", lambda e: e.tensor_scalar(out=xn[:, :], in0=X[:, :], scalar1=smA[:, 2:3], scalar2=None, op0=ALU.mult),
                 reads=[X_r, rstdA_r], writes=[xn_r])
            yield
            pt = nxt("ptr", 2)
            transposes(xn, xn_r, pt)
            for c in range(8):
                S.op("DVE", lambda e, c=c: e.tensor_scalar(out=uT[g % 2][:, c, :], in0=ptr[pt][:, c, :], scalar1=g1t[:, c:c + 1], scalar2=None,
                                                           op0=ALU.mult), reads=[ptr_r[pt], cr["g1t"]], writes=[uT_r[g % 2][c]])

        def stage_B(g):
            i = g % NT
            pb = g % 2
            pm = (g // 2) % 2
            cs = g % 2
            U, U_r = uT[pb], uT_r[pb]

            def proj(c0, ncols):
                p = nxt("pj", 2)
                for kc in range(8):
                    S.op("PE", lambda e, kc=kc: e.matmul(pj[p][:, 0:ncols], lhsT=U[:, kc, :], rhs=W[:, kc, c0:c0 + ncols],
                                                         start=(kc == 0), stop=(kc == 7)),
                         reads=[U_r[kc]] + W_r[kc][c0 // 1026:(c0 + ncols - 1) // 1026 + 1], writes=[pj_r[p]])
                    tick()
                return p

            evac = "ACT" if i < 16 else "DVE"

            def copy_from_psum(out_ap, in_ap, reads, writes):
                if evac == "ACT":
                    S.op("ACT", lambda e: e.activation(out=out_ap, in_=in_ap, func=AF.Copy), reads=reads, writes=writes)
                else:
                    S.op("DVE", lambda e: e.tensor_copy(out=out_ap, in_=in_ap), reads=reads, writes=writes)

            def rotary(p, col0):
                ps_ = pj[p][:, :].ap[0][0]
                x1 = bcast_ap(pj[p][:, 0:1], [[ps_, 128], [128, 4], [1, 64]])
                x2 = bcast_ap(pj[p][:, 64:65], [[ps_, 128], [128, 4], [1, 64]])
                cosv = bcast_ap(cst[:, cs, 0:1], [[256, 128], [0, 4], [1, 64]])
                sinv = bcast_ap(cst[:, cs, 64:65], [[256, 128], [0, 4], [1, 64]])
                t1 = bcast_ap(th[:, 0:1], [[512, 128], [64, 4], [1, 64]])
                t2 = bcast_ap(th[:, 256:257], [[512, 128], [64, 4], [1, 64]])
                o1 = bcast_ap(rot[:, col0:col0 + 1], [[1024, 128], [128, 4], [1, 64]])
                o2 = bcast_ap(rot[:, col0 + 64:col0 + 65], [[1024, 128], [128, 4], [1, 64]])
                rd = [pj_r[p], cst_r[cs]]
                oh = rot_r[col0 // 512]
                S.op("DVE", lambda e: e.tensor_tensor(out=t1, in0=x1, in1=cosv, op=ALU.mult), reads=rd, writes=[thA_r])
                S.op("DVE", lambda e: e.tensor_tensor(out=t2, in0=x2, in1=sinv, op=ALU.mult), reads=rd, writes=[thB_r])
                S.op("DVE", lambda e: e.tensor_tensor(out=o1, in0=t1, in1=t2, op=ALU.subtract), reads=[thA_r, thB_r], writes=[oh[0]])
                S.op("DVE", lambda e: e.tensor_tensor(out=t1, in0=x1, in1=sinv, op=ALU.mult), reads=rd, writes=[thA_r])
                S.op("DVE", lambda e: e.tensor_tensor(out=t2, in0=x2, in1=cosv, op=ALU.mult), reads=rd, writes=[thB_r])
                S.op("DVE", lambda e: e.tensor_tensor(out=o2, in0=t1, in1=t2, op=ALU.add), reads=[thA_r, thB_r], writes=[oh[1]])

            prq = proj(2056, 512)
            rotary(prq, 0)
            yield
            prk = proj(2568, 512)
            rotary(prk, 512)
            for h in range(4):
                S.op("DVE", lambda e, h=h: e.tensor_scalar(out=kd[:, h * 128:(h + 1) * 128], in0=rot[:, 512 + h * 128:512 + (h + 1) * 128],
                                                           scalar1=kdec[:, h:h + 1], scalar2=None, op0=ALU.mult),
                     reads=rot_r[1] + [cr["kdec"]], writes=[kd_r[h]])
            yield
            prv = proj(3080, 512)
            copy_from_psum(rv[:, :], pj[prv][:, :], [pj_r[prv]], [rv_r])
            prz = proj(3592, 512)
            S.op("ACT", lambda e: e.activation(out=th[:, :], in_=pj[prz][:, :], func=AF.Tanh, scale=0.5), reads=[pj_r[prz]], writes=th_r)
            S.op("DVE", lambda e: e.scalar_tensor_tensor(out=gret[:, :], in0=th[:, :], scalar=1.0, in1=pj[prz][:, :],
                                                         op0=ALU.add, op1=ALU.mult), reads=th_r + [pj_r[prz]], writes=[gret_r])
            yield
            pt2 = nxt("ptr", 2)
            transposes(rot, rot_r, pt2)
            S.op("DVE", lambda e: e.tensor_copy(out=rqT[:, :, :], in_=ptr[pt2][:, 0:4, :]), reads=[ptr_r[pt2]], writes=[rqT_r])
            S.op("DVE", lambda e: e.tensor_tensor(out=rqdT[:, :, :], in0=ptr[pt2][:, 0:4, :], in1=qdec[:, :, :], op=ALU.mult),
                 reads=[ptr_r[pt2], cr["qdec"]], writes=[rqdT_r])
            S.op("DVE", lambda e: e.tensor_copy(out=rkT[:, :, :], in_=ptr[pt2][:, 4:8, :]), reads=[ptr_r[pt2]], writes=[rkT_r])
            yield
            p = proj(2048, 8)
            ecol = 32 + 8 * (i % 2)
            S.op("DVE", lambda e, p=p: e.tensor_tensor(out=smB[:, ecol:ecol + 8], in0=pj[p][:, 0:8], in1=fbt[:, :], op=ALU.add),
                 reads=[pj_r[p], cr["fbt"]], writes=[lfe_r[i % 2]])
            pq = proj(0, 512)
            copy_from_psum(qk[:, 0:512], pj[pq][:, :], [pj_r[pq]], qk_r[0])
            S.op("ACT", lambda e: e.activation(out=smB[:, ecol:ecol + 8], in_=smB[:, ecol:ecol + 8], func=AF.Exp, scale=-1.0),
                 reads=[lfe_r[i % 2]], writes=[lfe_r[i % 2]])
            if i % 2 == 1:
                S.op("ACT", lambda e: e.activation(out=lf[:, :], in_=smB[:, 32:48], func=AF.Ln, bias=1.0),
                     reads=lfe_r, writes=[lf_r])
            pk = proj(512, 512)
            S.op("DVE", lambda e: e.tensor_copy(out=qk[:, 512:1024], in_=pj[pk][:, :]), reads=[pj_r[pk]], writes=qk_r[1])
            yield
            if i == 0:
                S.op("POOL", lambda e: e.memset(st_f[:, :, :].rearrange("p a b -> p (a b)"), 0.0), writes=st_f_r)
                S.op("POOL", lambda e: e.memset(st_b[:, :, :].rearrange("p a b -> p (a b)"), 0.0), writes=st_b_r)

            def ret_pair(hp):
                hs = (2 * hp, 2 * hp + 1)
                RBs = {}
                for h in hs:
                    rb = nxt("pj", 2)
                    RBs[h] = (pj[rb], pj_r[rb])
                sds = {}
                for h in hs:
                    RB, RB_r = RBs[h]
                    S.op("PE", lambda e, h=h, RB=RB: e.matmul(RB[:, 0:128], lhsT=rkT[:, h, :], rhs=rqT[:, h, :], start=True, stop=True),
                         reads=[rkT_r, rqT_r], writes=[RB_r])
                    S.op("PE", lambda e, h=h, RB=RB: e.matmul(RB[:, 256:384], lhsT=kd[:, h * 128:(h + 1) * 128], rhs=rv[:, h * 128:(h + 1) * 128],
                                                              start=True, stop=True), reads=[kd_r[h], rv_r], writes=[RB_r])
                for h in hs:
                    RB, RB_r = RBs[h]
                    sd = nxt("Sd", 2)
                    sds[h] = sd
                    S.op("DVE", lambda e, h=h, RB=RB, sd=sd: e.tensor_tensor(out=Sd[:, sd, :], in0=RB[:, 0:128], in1=dT[:, h, :], op=ALU.mult),
                         reads=[RB_r, cr["dT"]], writes=[Sd_r[sd]])
                    S.op("DVE", lambda e, h=h, RB=RB: e.scalar_tensor_tensor(out=st_f[:, h, :], in0=st_f[:, h, :], scalar=g128[h], in1=RB[:, 256:384],
                                                                             op0=ALU.mult, op1=ALU.add),
                         reads=[st_f_r[h], RB_r], writes=[st_f_r[h]])
                for h in hs:
                    RB, RB_r = RBs[h]
                    sd = sds[h]
                    S.op("PE", lambda e, h=h, RB=RB, sd=sd: e.matmul(RB[:, 128:256], lhsT=Sd[:, sd, :], rhs=rv[:, h * 128:(h + 1) * 128],
                                                                     start=True, stop=False),
                         reads=[Sd_r[sd], rv_r], writes=[RB_r])
                    S.op("PE", lambda e, h=h, RB=RB: e.matmul(RB[:, 128:256], lhsT=rqdT[:, h, :], rhs=st_b[:, h, :], start=False, stop=True),
                         reads=[rqdT_r, st_b_r[h]], writes=[RB_r])
                for h in hs:
                    RB, RB_r = RBs[h]
                    S.op("DVE", lambda e, h=h, RB=RB: e.bn_stats(out=bn[:, h, 0:6], in_=RB[:, 128:256]), reads=[RB_r], writes=[bn_r[h]])
                    S.op("DVE", lambda e, h=h: e.bn_aggr(out=bn[:, h, 6:8], in_=bn[:, h, 0:6]), reads=[bn_r[h]], writes=[bn_r[h]])
                    S.op("DVE", lambda e, h=h: e.tensor_scalar(out=smB[:, 24 + h:25 + h], in0=bn[:, h, 7:8], scalar1=GN_EPS, scalar2=None,
                                                               op0=ALU.add), reads=[bn_r[h]], writes=[gn_r[h]])
                for h in hs:
                    S.op("POOL", lambda e, h=h: e.tensor_tensor(out=smB[:, 16 + h:17 + h], in0=smB[:, 24 + h:25 + h],
                                                                in1=negh[:, :], op=ALU.pow), reads=[gn_r[h], cr["negh"]], writes=[gn_r[h]])
                    S.op("POOL", lambda e, h=h: e.tensor_copy(out=st_b[:, h, :], in_=st_f[:, h, :]), reads=[st_f_r[h]], writes=[st_b_r[h]])
                for h in hs:
                    S.op("DVE", lambda e, h=h: e.scalar_tensor_tensor(out=smB[:, 20 + h:21 + h], in0=bn[:, h, 6:7], scalar=-1.0,
                                                                      in1=smB[:, 16 + h:17 + h], op0=ALU.mult, op1=ALU.mult),
                         reads=[bn_r[h], gn_r[h]], writes=[gn_r[h]])
                for h in hs:
                    RB, RB_r = RBs[h]
                    if evac == "ACT":
                        S.op("ACT", lambda e, h=h, RB=RB: e.activation(out=RB[:, 128:256], in_=RB[:, 128:256], func=AF.Identity,
                                                                       scale=smB[:, 16 + h:17 + h], bias=smB[:, 20 + h:21 + h]),
                             reads=[RB_r, gn_r[h]], writes=[RB_r])
                    else:
                        S.op("DVE", lambda e, h=h, RB=RB: e.tensor_scalar(out=RB[:, 128:256], in0=RB[:, 128:256], scalar1=smB[:, 16 + h:17 + h],
                                                                          scalar2=smB[:, 20 + h:21 + h], op0=ALU.mult, op1=ALU.add),
                             reads=[RB_r, gn_r[h]], writes=[RB_r])
                for h in hs:
                    RB, RB_r = RBs[h]
                    S.op("DVE", lambda e, h=h, RB=RB: e.tensor_tensor(out=mixed[g % 4][:, 512 + h * 128:512 + (h + 1) * 128], in0=RB[:, 128:256],
                                                                      in1=gret[:, h * 128:(h + 1) * 128], op=ALU.mult),
                         reads=[RB_r, gret_r], writes=[mixr_r[g % 4][h]])

            ret_pair(0)
            yield
            cA, cM = carry_r[0], carry_r[1]
            if i == 0:
                S.op("DVE", lambda e: e.memset(carry[:, 0, :], 0.0), writes=[cA])
            if i % 2 == 1:
                pc_ = nxt("pj", 2)
                for t_ in range(2):
                    S.op("PE", lambda e, t_=t_: e.matmul(pj[pc_][:, 16 * t_:16 * t_ + 8], lhsT=triu[:, :], rhs=lf[:, 8 * t_:8 * t_ + 8],
                                                         start=True, stop=True), reads=[lf_r, cr["triu"]], writes=[pj_r[pc_]])
                    S.op("PE", lambda e, t_=t_: e.matmul(pj[pc_][:, 16 * t_ + 8:16 * t_ + 16], lhsT=onesf[:, :], rhs=lf[:, 8 * t_:8 * t_ + 8],
                                                         start=True, stop=True), reads=[lf_r, cr["ones"]], writes=[pj_r[pc_]])
                S.op("DVE", lambda e: e.tensor_tensor(out=a_all[:, i - 1, :], in0=pj[pc_][:, 0:8], in1=carry[:, 0, :], op=ALU.add),
                     reads=[pj_r[pc_], cA], writes=[aall_r])
                S.op("DVE", lambda e: e.tensor_tensor(out=carry[:, 1, :], in0=pj[pc_][:, 8:16], in1=carry[:, 0, :], op=ALU.add),
                     reads=[pj_r[pc_], cA], writes=[cM])
                S.op("DVE", lambda e: e.tensor_tensor(out=a_all[:, i, :], in0=pj[pc_][:, 16:24], in1=carry[:, 1, :], op=ALU.add),
                     reads=[pj_r[pc_], cM], writes=[aall_r])
                S.op("DVE", lambda e: e.tensor_tensor(out=carry[:, 0, :], in0=pj[pc_][:, 24:32], in1=carry[:, 1, :], op=ALU.add),
                     reads=[pj_r[pc_], cM], writes=[cA])
            pt = nxt("ptr", 2)
            transposes(qk, qk_r, pt)
            for e_ in range(2):
                S.op("DVE", lambda e, e_=e_: e.tensor_copy(out=QT[pm][e_ * 64:(e_ + 1) * 64, :, e_, (g % 2) * 128:(g % 2 + 1) * 128],
                                                           in_=ptr[pt][e_ * 64:(e_ + 1) * 64, 0:4, :]),
                     reads=[ptr_r[pt]], writes=[QT_r[pm][g % 2][e_]])
            S.op("DVE", lambda e: e.tensor_copy(out=KT[:, :, i * 128:(i + 1) * 128], in_=ptr[pt][:, 4:8, :]),
                 reads=[ptr_r[pt]], writes=[KT_r[i]])
            yield
            ret_pair(1)
            yield
            if i % 2 == 1:
                S.op("DVE", lambda e: e.tensor_scalar(out=smB[:, 8:16], in0=carry[:, 1, :], scalar1=-1.0, scalar2=None, op0=ALU.mult),
                     reads=[cM], writes=[aref_r])
                for h in range(FOX_HEADS):
                    S.op("DVE", lambda e, h=h: e.tensor_scalar(out=biasm[pm][:, 0:i + 1, h], in0=a_all[:, 0:i + 1, h],
                                                               scalar1=smB[:, 8 + h:9 + h], scalar2=None, op0=ALU.add),
                         reads=[aref_r, aall_r], writes=[biasm_r[pm][h]])
            pv = proj(1024, 512)
            copy_from_psum(VA[:, i, :, 0:64], pj[pv][:, :].rearrange("p (h d) -> p h d", h=FOX_HEADS), [pj_r[pv]], [VA_r[i]])
            yield
            pz = proj(1536, 512)
            S.op("ACT", lambda e: e.activation(out=th[:, :], in_=pj[pz][:, :], func=AF.Tanh, scale=0.5), reads=[pj_r[pz]], writes=th_r)
            S.op("DVE", lambda e: e.scalar_tensor_tensor(out=gfox[g % 4][:, :], in0=th[:, :], scalar=1.0, in1=pj[pz][:, :],
                                                         op0=ALU.add, op1=ALU.mult), reads=th_r + [pj_r[pz]], writes=[gfox_r[g % 4]])

        def stage_C(mg):
            g0 = 2 * mg
            i0 = g0 % NT
            i1 = i0 + 1
            pm = mg % 2
            blocks = [(h, j) for h in range(FOX_HEADS) for j in range(i1 + 1)]
            groups = [blocks[k:k + 2] for k in range(0, len(blocks), 2)]
            info = {}

            def emit_S(gi):
                sb_ = nxt("pS", 2)
                bank = pS[sb_][:, :, :].rearrange("p a b -> p (a b)")
                for q, (h, j) in enumerate(groups[gi]):
                    p_, e_ = h // 2, h % 2
                    dst = bank[:, q * 256:(q + 1) * 256]
                    lo = 128 if j == i1 else 0
                    diag = (j >= i0)
                    S.op("PE", lambda e, dst=dst, p_=p_, e_=e_, j=j, lo=lo, diag=diag: e.matmul(
                        dst[:, lo:256], lhsT=KT[:, p_, j * 128:(j + 1) * 128],
                        rhs=QT[pm][:, p_, e_, lo:256], start=True, stop=not diag),
                         reads=[KT_r[j], QT_r[pm][0][e_], QT_r[pm][1][e_]], writes=[pS_r[sb_]])
                    if diag:
                        S.op("PE", lambda e, dst=dst, lo=lo: e.matmul(dst[:, lo:lo + 128], lhsT=ident[:, :], rhs=maskneg[:, :],
                                                                     start=False, stop=True),
                             reads=[cr["ident"], cr["mask"]], writes=[pS_r[sb_]])
                for q, (h, j) in enumerate(groups[gi]):
                    dst = bank[:, q * 256:(q + 1) * 256]
                    lo = 128 if j == i1 else 0
                    pt_ = nxt("PT", NPT)
                    S.op("ACT", lambda e, dst=dst, pt_=pt_, h=h, j=j, lo=lo: e.activation(out=PT[:, pt_, lo:256], in_=dst[:, lo:256], func=AF.Exp,
                                                                                        scale=0.125, bias=biasm[pm][:, j, h:h + 1]),
                         reads=[pS_r[sb_], biasm_r[pm][h]], writes=[PT_r[pt_]])
                    info[(h, j)] = pt_

            def emit_PV(gi):
                for (h, j) in groups[gi]:
                    pt_ = info[(h, j)]
                    for q in range(2):
                        iq = i0 + q
                        if j > iq:
                            continue
                        ob = h % 2
                        S.op("PE", lambda e, pt_=pt_, h=h, j=j, q=q, iq=iq, ob=ob: e.matmul(
                            pO[ob][:, q * 128:q * 128 + 65], lhsT=PT[:, pt_, q * 128:(q + 1) * 128], rhs=VA[:, j, h, :],
                            start=(j == 0 and q == 0), stop=(j == iq), skip_group_check=True),
                             reads=[PT_r[pt_], VA_r[j]], writes=[pO_r[ob]])
                        if j == i1 and q == 1:
                            for qq in range(2):
                                mb = (g0 + qq) % 4
                                S.op("DVE", lambda e, h=h, qq=qq, ob=ob: e.reciprocal(out=smC[:, qq, h:h + 1],
                                                                                     in_=pO[ob][:, qq * 128 + 64:qq * 128 + 65]),
                                     reads=[pO_r[ob]], writes=[rl_r[qq]])
                            for qq in range(2):
                                mb = (g0 + qq) % 4
                                S.op("DVE", lambda e, h=h, qq=qq, mb=mb, ob=ob: e.scalar_tensor_tensor(
                                    out=mixed[mb][:, h * 64:(h + 1) * 64], in0=pO[ob][:, qq * 128:qq * 128 + 64],
                                    scalar=smC[:, qq, h:h + 1], in1=gfox[mb][:, h * 64:(h + 1) * 64], op0=ALU.mult, op1=ALU.mult),
                                     reads=[pO_r[ob], rl_r[qq], gfox_r[mb]], writes=[mixf_r[mb]])

            ng = len(groups)
            for gi in range(ng + 1):
                if gi < ng:
                    emit_S(gi)
                if gi - 1 >= 0:
                    emit_PV(gi - 1)
                yield

        def stage_D(g):
            pb = g % 4
            X, X_r = big[1], big_r[1]
            S.dma(lambda e: e.dma_start(out=X[:, :], in_=x_d[g * 128:(g + 1) * 128, :]), writes=[X_r])
            pt = nxt("ptr", 2)
            for c in range(8):
                S.op("PE", lambda e, c=c: e.transpose(out=ptr[pt][:, c, :], in_=mixed[pb][:, c * 128:(c + 1) * 128], identity=ident[:, :]),
                     reads=[mixf_r[pb]] + mixr_r[pb] + [cr["ident"]], writes=[ptr_r[pt]])
                tick()
            S.op("DVE", lambda e: e.tensor_scalar(out=mixT[:, :, :], in0=ptr[pt][:, :, :], scalar1=0.5, scalar2=None, op0=ALU.mult),
                 reads=[ptr_r[pt]], writes=[mixT_r])
            yield
            ph = []
            for half in range(2):
                p = nxt("pj", 2)
                for kc in range(8):
                    S.op("PE", lambda e, kc=kc, p=p, half=half: e.matmul(pj[p][:, :], lhsT=mixT[:, kc, :],
                                                                         rhs=WO[:, kc, half * 512:(half + 1) * 512],
                                                                         start=(kc == 0), stop=(kc == 7)),
                         reads=[mixT_r, WO_r[kc]], writes=[pj_r[p]])
                    tick()
                ph.append(p)
                S.op("ACT", lambda e, p=p, half=half: e.activation(out=junk[:, :], in_=pj[p][:, :], func=AF.Square,
                                                                   accum_out=smD[:, half:half + 1]),
                     reads=[pj_r[p]], writes=[junk_r[half], ssD_r[half]])
            S.op("DVE", lambda e: e.tensor_tensor(out=smD[:, 2:3], in0=smD[:, 0:1], in1=smD[:, 1:2], op=ALU.add),
                 reads=ssD_r, writes=[msD_r])
            S.op("DVE", lambda e: e.tensor_scalar(out=smD[:, 2:3], in0=smD[:, 2:3], scalar1=1.0 / D_MODEL,
                                                  scalar2=NORM_EPS, op0=ALU.mult, op1=ALU.add), reads=[msD_r], writes=[msD_r])
            S.op("POOL", lambda e: e.tensor_tensor(out=smD[:, 3:4], in0=smD[:, 2:3], in1=negh[:, :], op=ALU.pow),
                 reads=[msD_r, cr["negh"]], writes=[rstdD_r])
            for half in range(2):
                p = ph[half]
                S.op("DVE", lambda e, p=p, half=half: e.scalar_tensor_tensor(out=pj[p][:, :], in0=pj[p][:, :], scalar=smD[:, 3:4],
                                                                             in1=g2t[:, half * 512:(half + 1) * 512],
                                                                             op0=ALU.mult, op1=ALU.mult),
                     reads=[pj_r[p], rstdD_r, cr["g2t"]], writes=[pj_r[p]])
                S.op("DVE", lambda e, p=p, half=half: e.tensor_tensor(out=X[:, half * 512:(half + 1) * 512], in0=pj[p][:, :],
                                                                      in1=X[:, half * 512:(half + 1) * 512], op=ALU.add),
                     reads=[pj_r[p], X_r], writes=[X_r])
            if fill["fn"] is not None:
                fill["fn"](4)
            yield
            S.dma(lambda e: e.dma_start(out=y_d[g * 128:(g + 1) * 128, :], in_=X[:, :]), reads=[X_r])

        NG = NSEQ * NT
        assert NT % 2 == 0
        catt = None
        d_queue = []
        it = 0
        next_pair = 0
        ROUNDS = 12

        def c_steps(n):
            nonlocal catt
            for _ in range(n):
                if catt is None:
                    return
                try:
                    next(catt[1])
                    catt[4] += 1
                except StopIteration:
                    d_queue.extend([2 * catt[0], 2 * catt[0] + 1])
                    catt = None

        while True:
            gens = []
            if 0 <= it - 1 < NG:
                gens.append(stage_B(it - 1))
            if it < NG:
                gens.append(stage_A(it))
            if d_queue:
                gens.append(stage_D(d_queue.pop(0)))
            if catt is None and next_pair * 2 + 1 <= it - 2 and next_pair * 2 < NG:
                i1 = (2 * next_pair) % NT + 1
                catt = [next_pair, stage_C(next_pair), it, 4 * (i1 + 1) + 1, 0]
                next_pair += 1
            if it - 1 > 0 and (it - 1) % NT == 0 and it - 1 < NG + 1:
                c_steps(10 ** 6)
            k = 1
            fill["fn"] = c_steps
            fill["rate"] = 0.0
            fill["credit"] = 0.0
            if catt is not None:
                remaining = catt[3] - catt[4]
                target = remaining if (it - catt[2] >= 1 or it >= NG) else (catt[3] + 1) // 2
                fill["rate"] = max(0.0, (target - ROUNDS)) / 130.0
                k = 1
            while gens:
                first = True
                for gen in list(gens):
                    try:
                        next(gen)
                    except StopIteration:
                        gens.remove(gen)
                    if first:
                        c_steps(k)
                        first = False
            if catt is not None and (it - catt[2] >= 1 or it >= NG):
                c_steps(10 ** 6)
            it += 1
            if it > NG + 1 and catt is None and not d_queue and next_pair * 2 >= NG:
                break

        sem_names = ["PE", "ACT", "DVE", "POOL"]
        sems = {n: es.enter_context(nc.semaphore(f"s_{n}")) for n in sem_names}
        dsems = [es.enter_context(nc.semaphore(f"d_{k}")) for k in range(NDS)]
        qsems = [es.enter_context(nc.semaphore(f"q_{k}")) for k in range(S.n_qdma)]
        block = es.enter_context(nc.Block())
        S.emit(nc, block, sems, dsems, qsems)
    return nc


def kernel(x, pre_norm_gain, w_in, fox_forget_bias, w_out, post_norm_gain):
    x = np.asarray(x, np.float32)
    B, SEQ, D = x.shape
    NSEQ = B // N_CORES
    NT = SEQ // 128
    consts, g128 = make_consts(SEQ)
    nc = build_nc(NSEQ, NT, g128)
    shared = dict(
        w_in=np.ascontiguousarray(np.asarray(w_in, np.float32)[0]),
        w_out=np.ascontiguousarray(np.asarray(w_out, np.float32)[0]),
        g1t=np.ascontiguousarray(np.asarray(pre_norm_gain, np.float32)[0].reshape(8, 128).T),
        g2=np.ascontiguousarray(np.asarray(post_norm_gain, np.float32)[0].reshape(1, D)),
        fb=np.ascontiguousarray(np.asarray(fox_forget_bias, np.float32)[0].reshape(1, FOX_HEADS)),
        **consts,
    )
    in_maps = []
    for c in range(N_CORES):
        m = dict(shared)
        m["x"] = np.ascontiguousarray(x[c * NSEQ:(c + 1) * NSEQ].reshape(NSEQ * SEQ, D))
        in_maps.append(m)
    res = run_bass_kernel_spmd(nc, in_maps, core_ids=list(range(N_CORES)))
    out = np.concatenate([np.asarray(r["y"], np.float32).reshape(NSEQ, SEQ, D) for r in res.results], axis=0)
    return out
```

```python
import math
import os
from contextlib import ExitStack

import numpy as np
import ml_dtypes
import concourse.bass as bass
import concourse.mybir as mybir
from concourse.bass_utils import run_bass_kernel_spmd

F32 = mybir.dt.float32
BF16 = mybir.dt.bfloat16
ALU = mybir.AluOpType
AF = mybir.ActivationFunctionType

D_MODEL = 1024
D_IN = 4104
FOX_HEADS = 8
RET_HEADS = 4
NORM_EPS = 1e-6
GN_EPS = 1e-5
N_CORES = 8

SAME_ENGINE_SYNC = bool(int(os.environ.get("KSES", "1")))

STAGE = float(os.environ.get('KSTAGE', '99'))
NDS = 16
DUMP = bool(int(os.environ.get('KDUMP', '0')))


class Res:
    __slots__ = ("name", "last_w", "readers")

    def __init__(self, name):
        self.name = name
        self.last_w = None
        self.readers = []


class Op:
    __slots__ = ("fn", "deps", "dma_k", "needed")

    def __init__(self, fn, deps, dma_k=None):
        self.fn = fn
        self.deps = deps
        self.dma_k = dma_k
        self.needed = False


class Sched:
    ENGS = ("SP", "PE", "ACT", "DVE", "POOL")

    def __init__(self):
        self.ops = {e: [] for e in self.ENGS}
        self.n_dma = 0
        self.n_qdma = 0

    def op(self, eng, fn, reads=(), writes=()):
        deps = set()
        for r in reads:
            if r.last_w is not None:
                deps.add(r.last_w)
        for w in writes:
            if w.last_w is not None:
                deps.add(w.last_w)
            for rd in w.readers:
                deps.add(rd)
        idx = len(self.ops[eng])
        me = (eng, idx)
        deps.discard(me)
        self.ops[eng].append(Op(fn, deps))
        for r in reads:
            r.readers.append(me)
        for w in writes:
            w.last_w = me
            w.readers = []
        return me

    def dma(self, fn, reads=(), writes=(), eng="SP"):
        me = self.op(eng, fn, reads, writes)
        if eng == "SP":
            self.ops[eng][me[1]].dma_k = self.n_dma
            self.n_dma += 1
        else:
            self.ops[eng][me[1]].dma_k = -1 - self.n_qdma
            self.n_qdma += 1
        return me

    def emit(self, nc, block, sems, dsems, qsems):
        for e in self.ENGS:
            for o in self.ops[e]:
                for (e2, i2) in o.deps:
                    if self.ops[e2][i2].dma_k is not None:
                        continue
                    if e2 == e and (e == "PE" or not SAME_ENGINE_SYNC):
                        continue
                    self.ops[e2][i2].needed = True
        cnt = {}
        for e in self.ENGS:
            if e == "SP":
                continue
            c = 0
            arr = []
            for o in self.ops[e]:
                if o.needed:
                    c += 1
                arr.append(c)
            cnt[e] = arr

        def run(e, engobj):
            waited = {}
            for idx, o in enumerate(self.ops[e]):
                need = {}
                for (e2, i2) in o.deps:
                    if self.ops[e2][i2].dma_k is not None:
                        k = self.ops[e2][i2].dma_k
                        if k < 0:
                            key = ("Q", -1 - k)
                            val = 16
                        else:
                            key = ("D", k % NDS)
                            val = 16 * (k // NDS + 1)
                    else:
                        if e2 == e and (e == "PE" or not SAME_ENGINE_SYNC):
                            continue
                        key = ("C", e2)
                        val = cnt[e2][i2]
                    if val > need.get(key, 0):
                        need[key] = val
                if o.dma_k is not None and o.dma_k >= NDS:
                    key = ("D", o.dma_k % NDS)
                    val = 16 * (o.dma_k // NDS)
                    if val > need.get(key, 0):
                        need[key] = val
                for key, val in need.items():
                    if waited.get(key, 0) >= val:
                        continue
                    waited[key] = val
                    s = dsems[key[1]] if key[0] == "D" else (qsems[key[1]] if key[0] == "Q" else sems[key[1]])
                    engobj.wait_ge(s, val)
                    if DUMP:
                        print("   ", e, idx, "WAIT", key, val)
                ins = o.fn(engobj)
                if DUMP:
                    print(e, idx, "deps", sorted(o.deps), "inc" if o.needed else "", o.dma_k)
                if o.dma_k is not None and o.dma_k < 0:
                    ins.then_inc(qsems[-1 - o.dma_k], 16)
                elif o.dma_k is not None:
                    ins.then_inc(dsems[o.dma_k % NDS], 16)
                elif o.needed:
                    ins.then_inc(sems[e], 1)
            if e == "SP":
                for s in range(min(NDS, self.n_dma)):
                    last_k = ((self.n_dma - 1 - s) // NDS) * NDS + s
                    engobj.wait_ge(dsems[s], 16 * (last_k // NDS + 1))

        @block.sync
        def _(e):
            run("SP", e)

        @block.tensor
        def _(e):
            run("PE", e)

        @block.scalar
        def _(e):
            run("ACT", e)

        @block.vector
        def _(e):
            run("DVE", e)

        @block.gpsimd
        def _(e):
            run("POOL", e)


def bcast_ap(ap, dims):
    return bass.AP(tensor=ap.tensor, offset=ap.offset, ap=[list(d) for d in dims])


def make_consts(seq_len):
    lg = np.log1p(-np.exp(np.linspace(math.log(1.0 / 32), math.log(1.0 / 512), RET_HEADS))).astype(np.float32)
    lg64 = lg.astype(np.float64)
    ident = np.eye(128, dtype=np.float32).astype(ml_dtypes.bfloat16)
    s_idx = np.arange(128)[:, None]
    t_idx = np.arange(128)[None, :]
    maskneg = np.where(s_idx <= t_idx, 0.0, -30000.0).astype(np.float32).astype(ml_dtypes.bfloat16)
    triu = (s_idx <= t_idx).astype(np.float32)
    j = np.arange(128)[:, None]
    i = np.arange(128)[None, :]
    same = (i // 64) == (j // 64)
    fwd = (j // 64 == 0) & (i // 64 == 1)
    dT = np.zeros((128, RET_HEADS, 128), np.float32)
    for h in range(RET_HEADS):
        d = np.where(same, np.exp(lg64[h] * np.abs(i - j)), np.where(fwd, np.exp(lg64[h] * (i - j)), 0.0))
        dT[:, h, :] = (d * (128.0 ** -0.5)).astype(np.float32)
    il = np.arange(128, dtype=np.float64)
    qdec = np.stack([np.exp(lg64[h] * (il + 1.0)) for h in range(RET_HEADS)], 0).astype(np.float32)
    kdec = np.stack([np.exp(lg64[h] * (127.0 - il)) * (128.0 ** -0.5) for h in range(RET_HEADS)], 1).astype(np.float32)
    g128 = [float(np.exp(lg64[h] * 128.0)) for h in range(RET_HEADS)]
    half = 64
    inv = (1.0 / (10000.0 ** (np.arange(half, dtype=np.float32) / half))).astype(np.float32)
    ang = np.arange(seq_len, dtype=np.float32)[:, None] * inv[None, :]
    cossin = np.concatenate([np.cos(ang), np.sin(ang)], axis=1).astype(np.float32)
    return dict(ident=ident, maskneg=maskneg, triu=triu, dT=dT.reshape(128, RET_HEADS * 128),
                qdec=qdec.reshape(1, RET_HEADS * 128), kdec=kdec, cossin=cossin), g128


def build_nc(NSEQ, NT, g128):
    nc = bass.Bass("TRN2", target_bir_lowering=False, dynamic_dma_scratch_size=4096)
    NTOK = NSEQ * NT * 128
    SEQL = NT * 128
    x_d = nc.dram_tensor("x", [NTOK, D_MODEL], F32, kind="ExternalInput").ap()
    win_d = nc.dram_tensor("w_in", [D_MODEL, D_IN], F32, kind="ExternalInput").ap()
    wout_d = nc.dram_tensor("w_out", [D_MODEL, D_MODEL], F32, kind="ExternalInput").ap()
    g1t_d = nc.dram_tensor("g1t", [128, 8], F32, kind="ExternalInput").ap()
    g2_d = nc.dram_tensor("g2", [1, D_MODEL], F32, kind="ExternalInput").ap()
    fb_d = nc.dram_tensor("fb", [1, FOX_HEADS], F32, kind="ExternalInput").ap()
    ident_d = nc.dram_tensor("ident", [128, 128], BF16, kind="ExternalInput").ap()
    mask_d = nc.dram_tensor("maskneg", [128, 128], BF16, kind="ExternalInput").ap()
    triu_d = nc.dram_tensor("triu", [128, 128], F32, kind="ExternalInput").ap()
    dT_d = nc.dram_tensor("dT", [128, 512], F32, kind="ExternalInput").ap()
    qdec_d = nc.dram_tensor("qdec", [1, 512], F32, kind="ExternalInput").ap()
    kdec_d = nc.dram_tensor("kdec", [128, 4], F32, kind="ExternalInput").ap()
    cs_d = nc.dram_tensor("cossin", [SEQL, 128], F32, kind="ExternalInput").ap()
    y_d = nc.dram_tensor("y", [NTOK, D_MODEL], F32, kind="ExternalOutput").ap()

    S = Sched()
    es = ExitStack()

    def sb(name, shape, dt):
        return es.enter_context(nc.sbuf_tensor(name, shape, dt))

    def ps(name, shape, dt):
        return es.enter_context(nc.psum_tensor(name, shape, dt))

    with es:
        W = sb("W", [128, 8, D_IN], BF16)
        WO = sb("WO", [128, 8, D_MODEL], BF16)
        KT = sb("KT", [128, 4, SEQL], BF16)
        VA = sb("VA", [128, NT, FOX_HEADS, 65], BF16)
        KT_r = [Res(f"KT{t}") for t in range(NT)]
        VA_r = [Res(f"VA{t}") for t in range(NT)]
        a_all = sb("a_all", [128, NT, FOX_HEADS], F32)
        aall_r = Res("a_all")
        g2t = sb("g2t", [128, D_MODEL], F32)
        fbt = sb("fbt", [128, FOX_HEADS], F32)
        g1t = sb("g1t_s", [128, 8], F32)
        NBIG = 2
        big = [sb(f"big{i}", [128, 1024], F32) for i in range(NBIG)]
        big_r = [Res(f"big{i}") for i in range(NBIG)]
        xn = sb("xn", [128, 1024], BF16)
        xn_r = Res("xn")
        junk = sb("junk", [128, 512], BF16)
        _jr = Res("junk")
        junk_r = [_jr, _jr]
        uT = [sb(f"uT{k}", [128, 8, 128], BF16) for k in range(2)]
        uT_r = [[Res(f"uT{k}_{c}") for c in range(8)] for k in range(2)]
        qk = sb("qk", [128, 1024], BF16)
        qk_r = [[Res(f"qk{a}_{b}") for b in range(2)] for a in range(2)]
        QT = [sb(f"QT{k}", [128, 4, 2, 256], BF16) for k in range(2)]
        QT_r = [[[Res(f"QT{k}_{q}_{e}") for e in range(2)] for q in range(2)] for k in range(2)]
        gfox = [sb(f"gfox{k}", [128, 512], F32) for k in range(4)]
        gfox_r = [Res(f"gfox{k}") for k in range(4)]
        gret = sb("gret", [128, 512], F32)
        gret_r = Res("gret")
        th = sb("th", [128, 512], F32)
        thA_r, thB_r = Res("thA"), Res("thB")
        th_r = [thA_r, thB_r]
        rot, rot_r = qk, qk_r
        kd = sb("kd", [128, 512], BF16)
        kd_r = [Res(f"kd{h}") for h in range(4)]
        rqT = sb("rqT", [128, 4, 128], BF16)
        rkT = sb("rkT", [128, 4, 128], BF16)
        rqdT = sb("rqdT", [128, 4, 128], BF16)
        rqT_r, rkT_r, rqdT_r = Res("rqT"), Res("rkT"), Res("rqdT")
        rv = sb("rv", [128, 512], BF16)
        rv_r = Res("rv")
        st_f = sb("st_f", [128, 4, 128], F32)
        st_b = sb("st_b", [128, 4, 128], BF16)
        st_f_r = [Res(f"stf{h}") for h in range(4)]
        st_b_r = [Res(f"stb{h}") for h in range(4)]
        NPT = 6
        PT = sb("PT", [128, NPT, 256], BF16)
        PT_r = [Res(f"PT{i}") for i in range(NPT)]
        Sd = sb("Sd", [128, 2, 128], BF16)
        Sd_r = [Res("Sd0"), Res("Sd1")]
        mixed = [sb(f"mixed{k}", [128, 1024], BF16) for k in range(4)]
        mixf_r = [Res(f"mixf{k}") for k in range(4)]
        mixr_r = [[Res(f"mixr{k}_{h}") for h in range(4)] for k in range(4)]
        mixT = sb("mixT", [128, 8, 128], BF16)
        mixT_r = Res("mixT")
        ident = sb("ident_s", [128, 128], BF16)
        maskneg = sb("mask_s", [128, 128], BF16)
        triu = sb("triu_s", [128, 128], F32)
        onesf = sb("onesf", [128, 128], F32)
        dT = sb("dT_s", [128, 4, 128], F32)
        qdec = sb("qdec_s", [128, 4, 128], F32)
        kdec = sb("kdec_s", [128, 4], F32)
        cst = sb("cst", [128, 2, 128], F32)
        cst_r = [Res("cs0"), Res("cs1")]
        smA = sb("smA", [128, 8], F32)
        smB = sb("smB", [128, 64], F32)
        smC = sb("smC", [128, 2, 8], F32)
        smD = sb("smD", [128, 8], F32)
        negh = sb("negh", [128, 1], F32)
        bn = sb("bn", [128, 4, 8], F32)
        biasm = [sb(f"biasm{k}", [128, NT, FOX_HEADS], F32) for k in range(2)]
        biasm_r = [[Res(f"biasm{k}_{h}") for h in range(FOX_HEADS)] for k in range(2)]
        carry = sb("carry", [128, 2, FOX_HEADS], F32)
        lf = sb("lf", [128, 2 * FOX_HEADS], F32)
        lfe_r = [Res("lfe0"), Res("lfe1")]
        W_r = [[Res(f"W{kc}_{pc}") for pc in range(4)] for kc in range(8)]
        WO_r = [Res(f"WO{kc}") for kc in range(8)]
        cr = {n: Res("c_" + n) for n in ["ident", "mask", "triu", "dT", "qdec", "kdec", "g1t", "g2t", "fbt", "ones", "negh"]}
        ssA_r, msA_r, rstdA_r = Res("ssA"), Res("msA"), Res("rstdA")
        ssD_r, msD_r, rstdD_r = [Res("ssD0"), Res("ssD1")], Res("msD"), Res("rstdD")
        rl_r = [Res("rl0"), Res("rl1")]
        aref_r = Res("aref")
        gn_r = [Res(f"gn{h}") for h in range(4)]
        bn_r = [Res(f"bn{h}") for h in range(4)]
        carry_r = [Res("carry0"), Res("carry1")]
        lf_r = Res("lf")

        pj = [ps(f"pj{i}", [128, 512], F32) for i in range(2)]
        pj_r = [Res(f"pj{i}") for i in range(2)]
        ptr = [ps(f"ptr{i}", [128, 8, 128], BF16) for i in range(2)]
        ptr_r = [Res(f"ptr{i}") for i in range(2)]
        pS = [ps(f"pS{i}", [128, 4, 128], F32) for i in range(2)]
        pS_r = [Res(f"pS{i}") for i in range(2)]
        pO = [ps(f"pO{i}", [128, 512], F32) for i in range(2)]
        pO_r = [Res(f"pO{i}") for i in range(2)]

        cnt = {"pj": 0, "ptr": 0, "pS": 0, "PT": 0, "Sd": 0}

        def nxt(k, n):
            v = cnt[k] % n
            cnt[k] += 1
            return v

        S.dma(lambda e: e.dma_start(out=ident[:, :], in_=ident_d[:, :]), writes=[cr["ident"]])
        S.dma(lambda e: e.dma_start(out=maskneg[:, :], in_=mask_d[:, :]), writes=[cr["mask"]])
        S.dma(lambda e: e.dma_start(out=triu[:, :], in_=triu_d[:, :]), writes=[cr["triu"]])
        S.dma(lambda e: e.dma_start(out=dT[:, :, :].rearrange("p a b -> p (a b)"), in_=dT_d[:, :]), writes=[cr["dT"]])
        S.dma(lambda e: e.dma_start(out=qdec[:, :, :].rearrange("p a b -> p (a b)"),
                                    in_=bcast_ap(qdec_d, [[0, 128], [1, 512]])), writes=[cr["qdec"]])
        S.dma(lambda e: e.dma_start(out=kdec[:, :], in_=kdec_d[:, :]), writes=[cr["kdec"]])
        S.dma(lambda e: e.dma_start(out=g1t[:, :], in_=g1t_d[:, :]), writes=[cr["g1t"]])
        S.dma(lambda e: e.dma_start(out=g2t[:, :], in_=bcast_ap(g2_d, [[0, 128], [1, D_MODEL]])), writes=[cr["g2t"]])
        S.dma(lambda e: e.dma_start(out=fbt[:, :], in_=bcast_ap(fb_d, [[0, 128], [1, FOX_HEADS]])), writes=[cr["fbt"]])
        S.op("POOL", lambda e: e.memset(onesf[:, :], 1.0), writes=[cr["ones"]])
        S.op("POOL", lambda e: e.memset(negh[:, :], -0.5), writes=[cr["negh"]])
        S.op("POOL", lambda e: e.memset(VA[:, :, :, :].rearrange("p a b c -> p (a b c)"), 1.0), writes=VA_r)
        for k in range(2):
            S.op("POOL", lambda e, k=k: e.memset(QT[k][:, :, :, :].rearrange("p a b c -> p (a b c)"), 0.0),
                 writes=[QT_r[k][q][e_] for q in range(2) for e_ in range(2)])

        nb = 0
        for pc in (2, 3, 0, 1):
            for kc in range(8):
                c0 = pc * 1026
                S.dma(lambda e, kc=kc, c0=c0: e.dma_start(out=W[:, kc, c0:c0 + 1026], in_=win_d[kc * 128:(kc + 1) * 128, c0:c0 + 1026],
                                                          max_dma_last_dim=4104),
                      writes=[W_r[kc][pc]], eng="POOL")
        for kc in range(8):
            S.dma(lambda e, kc=kc: e.dma_start(out=WO[:, kc, :], in_=wout_d[kc * 128:(kc + 1) * 128, :], max_dma_last_dim=4096),
                  writes=[WO_r[kc]], eng="POOL")

        fill = {"rate": 0.0, "credit": 0.0, "fn": None}

        def tick(n=1):
            if fill["fn"] is None or fill["rate"] <= 0.0:
                return
            fill["credit"] += n * fill["rate"]
            if fill["credit"] >= 1.0:
                k_ = int(fill["credit"])
                fill["credit"] -= k_
                fill["fn"](k_)

        def transposes(src, src_r, pt):
            for c in range(8):
                S.op("PE", lambda e, c=c: e.transpose(out=ptr[pt][:, c, :], in_=src[:, c * 128:(c + 1) * 128], identity=ident[:, :]),
                     reads=(src_r[c // 4] if isinstance(src_r, list) else [src_r]) + [cr["ident"]], writes=[ptr_r[pt]])
                tick()

        def stage_A(g):
            i = g % NT
            X, X_r = big[0], big_r[0]
            S.dma(lambda e: e.dma_start(out=X[:, :], in_=x_d[g * 128:(g + 1) * 128, :]), writes=[X_r])
            S.dma(lambda e: e.dma_start(out=cst[:, g % 2, :], in_=cs_d[i * 128:(i + 1) * 128, :]), writes=[cst_r[g % 2]])
            yield
            S.op("ACT", lambda e: e.activation(out=xn[:, :], in_=X[:, :], func=AF.Square, accum_out=smA[:, 0:1]),
                 reads=[X_r], writes=[xn_r, ssA_r])
            yield
            S.op("DVE", lambda e: e.tensor_scalar(out=smA[:, 1:2], in0=smA[:, 0:1], scalar1=1.0 / D_MODEL,
                                                  scalar2=NORM_EPS, op0=ALU.mult, op1=ALU.add), reads=[ssA_r], writes=[msA_r])
            yield
            S.op("POOL", lambda e: e.tensor_tensor(out=smA[:, 2:3], in0=smA[:, 1:2], in1=negh[:, :], op=ALU.pow),
                 reads=[msA_r, cr["negh"]], writes=[rstdA_r])
            yield
            S.op("DVE", lambda e: e.tensor_scalar(out=xn[:, :], in0=X[:, :], scalar1=smA[:, 2:3], scalar2=None, op0=ALU.mult),
                 reads=[X_r, rstdA_r], writes=[xn_r])
            yield
            pt = nxt("ptr", 2)
            transposes(xn, xn_r, pt)
            for c in range(8):
                S.op("DVE", lambda e, c=c: e.tensor_scalar(out=uT[g % 2][:, c, :], in0=ptr[pt][:, c, :], scalar1=g1t[:, c:c + 1], scalar2=None,
                                                           op0=ALU.mult), reads=[ptr_r[pt], cr["g1t"]], writes=[uT_r[g % 2][c]])

        def stage_B(g):
            i = g % NT
            pb = g % 2
            pm = (g // 2) % 2
            cs = g % 2
            U, U_r = uT[pb], uT_r[pb]

            def proj(c0, ncols):
                p = nxt("pj", 2)
                for kc in range(8):
                    S.op("PE", lambda e, kc=kc: e.matmul(pj[p][:, 0:ncols], lhsT=U[:, kc, :], rhs=W[:, kc, c0:c0 + ncols],
                                                         start=(kc == 0), stop=(kc == 7)),
                         reads=[U_r[kc]] + W_r[kc][c0 // 1026:(c0 + ncols - 1) // 1026 + 1], writes=[pj_r[p]])
                    tick()
                return p

            evac = "ACT" if i < 10 else "DVE"

            def copy_from_psum(out_ap, in_ap, reads, writes):
                if evac == "ACT":
                    S.op("ACT", lambda e: e.activation(out=out_ap, in_=in_ap, func=AF.Copy), reads=reads, writes=writes)
                else:
                    S.op("DVE", lambda e: e.tensor_copy(out=out_ap, in_=in_ap), reads=reads, writes=writes)

            def rotary(p, col0):
                ps_ = pj[p][:, :].ap[0][0]
                x1 = bcast_ap(pj[p][:, 0:1], [[ps_, 128], [128, 4], [1, 64]])
                x2 = bcast_ap(pj[p][:, 64:65], [[ps_, 128], [128, 4], [1, 64]])
                cosv = bcast_ap(cst[:, cs, 0:1], [[256, 128], [0, 4], [1, 64]])
                sinv = bcast_ap(cst[:, cs, 64:65], [[256, 128], [0, 4], [1, 64]])
                t1 = bcast_ap(th[:, 0:1], [[512, 128], [64, 4], [1, 64]])
                t2 = bcast_ap(th[:, 256:257], [[512, 128], [64, 4], [1, 64]])
                o1 = bcast_ap(rot[:, col0:col0 + 1], [[1024, 128], [128, 4], [1, 64]])
                o2 = bcast_ap(rot[:, col0 + 64:col0 + 65], [[1024, 128], [128, 4], [1, 64]])
                rd = [pj_r[p], cst_r[cs]]
                oh = rot_r[col0 // 512]
                S.op("DVE", lambda e: e.tensor_tensor(out=t1, in0=x1, in1=cosv, op=ALU.mult), reads=rd, writes=[thA_r])
                S.op("DVE", lambda e: e.tensor_tensor(out=t2, in0=x2, in1=sinv, op=ALU.mult), reads=rd, writes=[thB_r])
                S.op("DVE", lambda e: e.tensor_tensor(out=o1, in0=t1, in1=t2, op=ALU.subtract), reads=[thA_r, thB_r], writes=[oh[0]])
                S.op("DVE", lambda e: e.tensor_tensor(out=t1, in0=x1, in1=sinv, op=ALU.mult), reads=rd, writes=[thA_r])
                S.op("DVE", lambda e: e.tensor_tensor(out=t2, in0=x2, in1=cosv, op=ALU.mult), reads=rd, writes=[thB_r])
                S.op("DVE", lambda e: e.tensor_tensor(out=o2, in0=t1, in1=t2, op=ALU.add), reads=[thA_r, thB_r], writes=[oh[1]])

            prq = proj(2056, 512)
            rotary(prq, 0)
            yield
            prk = proj(2568, 512)
            rotary(prk, 512)
            for h in range(4):
                S.op("DVE", lambda e, h=h: e.tensor_scalar(out=kd[:, h * 128:(h + 1) * 128], in0=rot[:, 512 + h * 128:512 + (h + 1) * 128],
                                                           scalar1=kdec[:, h:h + 1], scalar2=None, op0=ALU.mult),
                     reads=rot_r[1] + [cr["kdec"]], writes=[kd_r[h]])
            yield
            prv = proj(3080, 512)
            copy_from_psum(rv[:, :], pj[prv][:, :], [pj_r[prv]], [rv_r])
            prz = proj(3592, 512)
            S.op("ACT", lambda e: e.activation(out=th[:, :], in_=pj[prz][:, :], func=AF.Tanh, scale=0.5), reads=[pj_r[prz]], writes=th_r)
            S.op("DVE", lambda e: e.scalar_tensor_tensor(out=gret[:, :], in0=th[:, :], scalar=1.0, in1=pj[prz][:, :],
                                                         op0=ALU.add, op1=ALU.mult), reads=th_r + [pj_r[prz]], writes=[gret_r])
            yield
            pt2 = nxt("ptr", 2)
            transposes(rot, rot_r, pt2)
            S.op("DVE", lambda e: e.tensor_copy(out=rqT[:, :, :], in_=ptr[pt2][:, 0:4, :]), reads=[ptr_r[pt2]], writes=[rqT_r])
            S.op("DVE", lambda e: e.tensor_tensor(out=rqdT[:, :, :], in0=ptr[pt2][:, 0:4, :], in1=qdec[:, :, :], op=ALU.mult),
                 reads=[ptr_r[pt2], cr["qdec"]], writes=[rqdT_r])
            S.op("DVE", lambda e: e.tensor_copy(out=rkT[:, :, :], in_=ptr[pt2][:, 4:8, :]), reads=[ptr_r[pt2]], writes=[rkT_r])
            yield
            p = proj(2048, 8)
            ecol = 32 + 8 * (i % 2)
            S.op("DVE", lambda e, p=p: e.tensor_tensor(out=smB[:, ecol:ecol + 8], in0=pj[p][:, 0:8], in1=fbt[:, :], op=ALU.add),
                 reads=[pj_r[p], cr["fbt"]], writes=[lfe_r[i % 2]])
            pq = proj(0, 512)
            copy_from_psum(qk[:, 0:512], pj[pq][:, :], [pj_r[pq]], qk_r[0])
            S.op("ACT", lambda e: e.activation(out=smB[:, ecol:ecol + 8], in_=smB[:, ecol:ecol + 8], func=AF.Exp, scale=-1.0),
                 reads=[lfe_r[i % 2]], writes=[lfe_r[i % 2]])
            if i % 2 == 1:
                S.op("ACT", lambda e: e.activation(out=lf[:, :], in_=smB[:, 32:48], func=AF.Ln, bias=1.0),
                     reads=lfe_r, writes=[lf_r])
            pk = proj(512, 512)
            S.op("DVE", lambda e: e.tensor_copy(out=qk[:, 512:1024], in_=pj[pk][:, :]), reads=[pj_r[pk]], writes=qk_r[1])
            yield
            if i == 0:
                S.op("POOL", lambda e: e.memset(st_f[:, :, :].rearrange("p a b -> p (a b)"), 0.0), writes=st_f_r)
                S.op("POOL", lambda e: e.memset(st_b[:, :, :].rearrange("p a b -> p (a b)"), 0.0), writes=st_b_r)

            def ret_pair(hp):
                hs = (2 * hp, 2 * hp + 1)
                RBs = {}
                for h in hs:
                    rb = nxt("pj", 2)
                    RBs[h] = (pj[rb], pj_r[rb])
                sds = {}
                for h in hs:
                    RB, RB_r = RBs[h]
                    S.op("PE", lambda e, h=h, RB=RB: e.matmul(RB[:, 0:128], lhsT=rkT[:, h, :], rhs=rqT[:, h, :], start=True, stop=True),
                         reads=[rkT_r, rqT_r], writes=[RB_r])
                    S.op("PE", lambda e, h=h, RB=RB: e.matmul(RB[:, 256:384], lhsT=kd[:, h * 128:(h + 1) * 128], rhs=rv[:, h * 128:(h + 1) * 128],
                                                              start=True, stop=True), reads=[kd_r[h], rv_r], writes=[RB_r])
                for h in hs:
                    RB, RB_r = RBs[h]
                    sd = nxt("Sd", 2)
                    sds[h] = sd
                    S.op("DVE", lambda e, h=h, RB=RB, sd=sd: e.tensor_tensor(out=Sd[:, sd, :], in0=RB[:, 0:128], in1=dT[:, h, :], op=ALU.mult),
                         reads=[RB_r, cr["dT"]], writes=[Sd_r[sd]])
                    S.op("DVE", lambda e, h=h, RB=RB: e.scalar_tensor_tensor(out=st_f[:, h, :], in0=st_f[:, h, :], scalar=g128[h], in1=RB[:, 256:384],
                                                                             op0=ALU.mult, op1=ALU.add),
                         reads=[st_f_r[h], RB_r], writes=[st_f_r[h]])
                for h in hs:
                    RB, RB_r = RBs[h]
                    sd = sds[h]
                    S.op("PE", lambda e, h=h, RB=RB, sd=sd: e.matmul(RB[:, 128:256], lhsT=Sd[:, sd, :], rhs=rv[:, h * 128:(h + 1) * 128],
                                                                     start=True, stop=False),
                         reads=[Sd_r[sd], rv_r], writes=[RB_r])
                    S.op("PE", lambda e, h=h, RB=RB: e.matmul(RB[:, 128:256], lhsT=rqdT[:, h, :], rhs=st_b[:, h, :], start=False, stop=True),
                         reads=[rqdT_r, st_b_r[h]], writes=[RB_r])
                for h in hs:
                    RB, RB_r = RBs[h]
                    S.op("DVE", lambda e, h=h, RB=RB: e.bn_stats(out=bn[:, h, 0:6], in_=RB[:, 128:256]), reads=[RB_r], writes=[bn_r[h]])
                    S.op("DVE", lambda e, h=h: e.bn_aggr(out=bn[:, h, 6:8], in_=bn[:, h, 0:6]), reads=[bn_r[h]], writes=[bn_r[h]])
                    S.op("DVE", lambda e, h=h: e.tensor_scalar(out=smB[:, 24 + h:25 + h], in0=bn[:, h, 7:8], scalar1=GN_EPS, scalar2=None,
                                                               op0=ALU.add), reads=[bn_r[h]], writes=[gn_r[h]])
                for h in hs:
                    S.op("POOL", lambda e, h=h: e.tensor_tensor(out=smB[:, 16 + h:17 + h], in0=smB[:, 24 + h:25 + h],
                                                                in1=negh[:, :], op=ALU.pow), reads=[gn_r[h], cr["negh"]], writes=[gn_r[h]])
                    S.op("POOL", lambda e, h=h: e.tensor_copy(out=st_b[:, h, :], in_=st_f[:, h, :]), reads=[st_f_r[h]], writes=[st_b_r[h]])
                for h in hs:
                    S.op("DVE", lambda e, h=h: e.scalar_tensor_tensor(out=smB[:, 20 + h:21 + h], in0=bn[:, h, 6:7], scalar=-1.0,
                                                                      in1=smB[:, 16 + h:17 + h], op0=ALU.mult, op1=ALU.mult),
                         reads=[bn_r[h], gn_r[h]], writes=[gn_r[h]])
                for h in hs:
                    RB, RB_r = RBs[h]
                    if evac == "ACT":
                        S.op("ACT", lambda e, h=h, RB=RB: e.activation(out=RB[:, 128:256], in_=RB[:, 128:256], func=AF.Identity,
                                                                       scale=smB[:, 16 + h:17 + h], bias=smB[:, 20 + h:21 + h]),
                             reads=[RB_r, gn_r[h]], writes=[RB_r])
                    else:
                        S.op("DVE", lambda e, h=h, RB=RB: e.tensor_scalar(out=RB[:, 128:256], in0=RB[:, 128:256], scalar1=smB[:, 16 + h:17 + h],
                                                                          scalar2=smB[:, 20 + h:21 + h], op0=ALU.mult, op1=ALU.add),
                             reads=[RB_r, gn_r[h]], writes=[RB_r])
                for h in hs:
                    RB, RB_r = RBs[h]
                    S.op("DVE", lambda e, h=h, RB=RB: e.tensor_tensor(out=mixed[g % 4][:, 512 + h * 128:512 + (h + 1) * 128], in0=RB[:, 128:256],
                                                                      in1=gret[:, h * 128:(h + 1) * 128], op=ALU.mult),
                         reads=[RB_r, gret_r], writes=[mixr_r[g % 4][h]])

            ret_pair(0)
            yield
            cA, cM = carry_r[0], carry_r[1]
            if i == 0:
                S.op("DVE", lambda e: e.memset(carry[:, 0, :], 0.0), writes=[cA])
            if i % 2 == 1:
                pc_ = nxt("pj", 2)
                for t_ in range(2):
                    S.op("PE", lambda e, t_=t_: e.matmul(pj[pc_][:, 16 * t_:16 * t_ + 8], lhsT=triu[:, :], rhs=lf[:, 8 * t_:8 * t_ + 8],
                                                         start=True, stop=True), reads=[lf_r, cr["triu"]], writes=[pj_r[pc_]])
                    S.op("PE", lambda e, t_=t_: e.matmul(pj[pc_][:, 16 * t_ + 8:16 * t_ + 16], lhsT=onesf[:, :], rhs=lf[:, 8 * t_:8 * t_ + 8],
                                                         start=True, stop=True), reads=[lf_r, cr["ones"]], writes=[pj_r[pc_]])
                S.op("DVE", lambda e: e.tensor_tensor(out=a_all[:, i - 1, :], in0=pj[pc_][:, 0:8], in1=carry[:, 0, :], op=ALU.add),
                     reads=[pj_r[pc_], cA], writes=[aall_r])
                S.op("DVE", lambda e: e.tensor_tensor(out=carry[:, 1, :], in0=pj[pc_][:, 8:16], in1=carry[:, 0, :], op=ALU.add),
                     reads=[pj_r[pc_], cA], writes=[cM])
                S.op("DVE", lambda e: e.tensor_tensor(out=a_all[:, i, :], in0=pj[pc_][:, 16:24], in1=carry[:, 1, :], op=ALU.add),
                     reads=[pj_r[pc_], cM], writes=[aall_r])
                S.op("DVE", lambda e: e.tensor_tensor(out=carry[:, 0, :], in0=pj[pc_][:, 24:32], in1=carry[:, 1, :], op=ALU.add),
                     reads=[pj_r[pc_], cM], writes=[cA])
            pt = nxt("ptr", 2)
            transposes(qk, qk_r, pt)
            for e_ in range(2):
                S.op("DVE", lambda e, e_=e_: e.tensor_copy(out=QT[pm][e_ * 64:(e_ + 1) * 64, :, e_, (g % 2) * 128:(g % 2 + 1) * 128],
                                                           in_=ptr[pt][e_ * 64:(e_ + 1) * 64, 0:4, :]),
                     reads=[ptr_r[pt]], writes=[QT_r[pm][g % 2][e_]])
            S.op("DVE", lambda e: e.tensor_copy(out=KT[:, :, i * 128:(i + 1) * 128], in_=ptr[pt][:, 4:8, :]),
                 reads=[ptr_r[pt]], writes=[KT_r[i]])
            yield
            ret_pair(1)
            yield
            if i % 2 == 1:
                S.op("DVE", lambda e: e.tensor_scalar(out=smB[:, 8:16], in0=carry[:, 1, :], scalar1=-1.0, scalar2=None, op0=ALU.mult),
                     reads=[cM], writes=[aref_r])
                for h in range(FOX_HEADS):
                    S.op("DVE", lambda e, h=h: e.tensor_scalar(out=biasm[pm][:, 0:i + 1, h], in0=a_all[:, 0:i + 1, h],
                                                               scalar1=smB[:, 8 + h:9 + h], scalar2=None, op0=ALU.add),
                         reads=[aref_r, aall_r], writes=[biasm_r[pm][h]])
            pv = proj(1024, 512)
            copy_from_psum(VA[:, i, :, 0:64], pj[pv][:, :].rearrange("p (h d) -> p h d", h=FOX_HEADS), [pj_r[pv]], [VA_r[i]])
            yield
            pz = proj(1536, 512)
            S.op("ACT", lambda e: e.activation(out=th[:, :], in_=pj[pz][:, :], func=AF.Tanh, scale=0.5), reads=[pj_r[pz]], writes=th_r)
            S.op("DVE", lambda e: e.scalar_tensor_tensor(out=gfox[g % 4][:, :], in0=th[:, :], scalar=1.0, in1=pj[pz][:, :],
                                                         op0=ALU.add, op1=ALU.mult), reads=th_r + [pj_r[pz]], writes=[gfox_r[g % 4]])

        def stage_C(mg):
            g0 = 2 * mg
            i0 = g0 % NT
            i1 = i0 + 1
            pm = mg % 2
            blocks = [(h, j) for h in range(FOX_HEADS) for j in range(i1 + 1)]
            groups = [blocks[k:k + 2] for k in range(0, len(blocks), 2)]
            info = {}

            def emit_S(gi):
                sb_ = nxt("pS", 2)
                bank = pS[sb_][:, :, :].rearrange("p a b -> p (a b)")
                for q, (h, j) in enumerate(groups[gi]):
                    p_, e_ = h // 2, h % 2
                    dst = bank[:, q * 256:(q + 1) * 256]
                    lo = 128 if j == i1 else 0
                    diag = (j >= i0)
                    S.op("PE", lambda e, dst=dst, p_=p_, e_=e_, j=j, lo=lo, diag=diag: e.matmul(
                        dst[:, lo:256], lhsT=KT[:, p_, j * 128:(j + 1) * 128],
                        rhs=QT[pm][:, p_, e_, lo:256], start=True, stop=not diag),
                         reads=[KT_r[j], QT_r[pm][0][e_], QT_r[pm][1][e_]], writes=[pS_r[sb_]])
                    if diag:
                        S.op("PE", lambda e, dst=dst, lo=lo: e.matmul(dst[:, lo:lo + 128], lhsT=ident[:, :], rhs=maskneg[:, :],
                                                                     start=False, stop=True),
                             reads=[cr["ident"], cr["mask"]], writes=[pS_r[sb_]])
                for q, (h, j) in enumerate(groups[gi]):
                    dst = bank[:, q * 256:(q + 1) * 256]
                    lo = 128 if j == i1 else 0
                    pt_ = nxt("PT", NPT)
                    S.op("ACT", lambda e, dst=dst, pt_=pt_, h=h, j=j, lo=lo: e.activation(out=PT[:, pt_, lo:256], in_=dst[:, lo:256], func=AF.Exp,
                                                                                        scale=0.125, bias=biasm[pm][:, j, h:h + 1]),
                         reads=[pS_r[sb_], biasm_r[pm][h]], writes=[PT_r[pt_]])
                    info[(h, j)] = pt_

            def emit_PV(gi):
                for (h, j) in groups[gi]:
                    pt_ = info[(h, j)]
                    for q in range(2):
                        iq = i0 + q
                        if j > iq:
                            continue
                        ob = h % 2
                        S.op("PE", lambda e, pt_=pt_, h=h, j=j, q=q, iq=iq, ob=ob: e.matmul(
                            pO[ob][:, q * 128:q * 128 + 65], lhsT=PT[:, pt_, q * 128:(q + 1) * 128], rhs=VA[:, j, h, :],
                            start=(j == 0 and q == 0), stop=(j == iq), skip_group_check=True),
                             reads=[PT_r[pt_], VA_r[j]], writes=[pO_r[ob]])
                        if j == i1 and q == 1:
                            for qq in range(2):
                                mb = (g0 + qq) % 4
                                S.op("DVE", lambda e, h=h, qq=qq, ob=ob: e.reciprocal(out=smC[:, qq, h:h + 1],
                                                                                     in_=pO[ob][:, qq * 128 + 64:qq * 128 + 65]),
                                     reads=[pO_r[ob]], writes=[rl_r[qq]])
                            for qq in range(2):
                                mb = (g0 + qq) % 4
                                S.op("DVE", lambda e, h=h, qq=qq, mb=mb, ob=ob: e.scalar_tensor_tensor(
                                    out=mixed[mb][:, h * 64:(h + 1) * 64], in0=pO[ob][:, qq * 128:qq * 128 + 64],
                                    scalar=smC[:, qq, h:h + 1], in1=gfox[mb][:, h * 64:(h + 1) * 64], op0=ALU.mult, op1=ALU.mult),
                                     reads=[pO_r[ob], rl_r[qq], gfox_r[mb]], writes=[mixf_r[mb]])

            ng = len(groups)
            for gi in range(ng + 1):
                if gi < ng:
                    emit_S(gi)
                if gi - 1 >= 0:
                    emit_PV(gi - 1)
                yield

        def stage_D(g):
            pb = g % 4
            X, X_r = big[1], big_r[1]
            S.dma(lambda e: e.dma_start(out=X[:, :], in_=x_d[g * 128:(g + 1) * 128, :]), writes=[X_r])
            pt = nxt("ptr", 2)
            for c in range(8):
                S.op("PE", lambda e, c=c: e.transpose(out=ptr[pt][:, c, :], in_=mixed[pb][:, c * 128:(c + 1) * 128], identity=ident[:, :]),
                     reads=[mixf_r[pb]] + mixr_r[pb] + [cr["ident"]], writes=[ptr_r[pt]])
                tick()
            S.op("DVE", lambda e: e.tensor_scalar(out=mixT[:, :, :], in0=ptr[pt][:, :, :], scalar1=0.5, scalar2=None, op0=ALU.mult),
                 reads=[ptr_r[pt]], writes=[mixT_r])
            yield
            ph = []
            for half in range(2):
                p = nxt("pj", 2)
                for kc in range(8):
                    S.op("PE", lambda e, kc=kc, p=p, half=half: e.matmul(pj[p][:, :], lhsT=mixT[:, kc, :],
                                                                         rhs=WO[:, kc, half * 512:(half + 1) * 512],
                                                                         start=(kc == 0), stop=(kc == 7)),
                         reads=[mixT_r, WO_r[kc]], writes=[pj_r[p]])
                    tick()
                ph.append(p)
                S.op("ACT", lambda e, p=p, half=half: e.activation(out=junk[:, :], in_=pj[p][:, :], func=AF.Square,
                                                                   accum_out=smD[:, half:half + 1]),
                     reads=[pj_r[p]], writes=[junk_r[half], ssD_r[half]])
            S.op("DVE", lambda e: e.tensor_tensor(out=smD[:, 2:3], in0=smD[:, 0:1], in1=smD[:, 1:2], op=ALU.add),
                 reads=ssD_r, writes=[msD_r])
            S.op("DVE", lambda e: e.tensor_scalar(out=smD[:, 2:3], in0=smD[:, 2:3], scalar1=1.0 / D_MODEL,
                                                  scalar2=NORM_EPS, op0=ALU.mult, op1=ALU.add), reads=[msD_r], writes=[msD_r])
            S.op("POOL", lambda e: e.tensor_tensor(out=smD[:, 3:4], in0=smD[:, 2:3], in1=negh[:, :], op=ALU.pow),
                 reads=[msD_r, cr["negh"]], writes=[rstdD_r])
            for half in range(2):
                p = ph[half]
                S.op("DVE", lambda e, p=p, half=half: e.scalar_tensor_tensor(out=pj[p][:, :], in0=pj[p][:, :], scalar=smD[:, 3:4],
                                                                             in1=g2t[:, half * 512:(half + 1) * 512],
                                                                             op0=ALU.mult, op1=ALU.mult),
                     reads=[pj_r[p], rstdD_r, cr["g2t"]], writes=[pj_r[p]])
                S.op("DVE", lambda e, p=p, half=half: e.tensor_tensor(out=X[:, half * 512:(half + 1) * 512], in0=pj[p][:, :],
                                                                      in1=X[:, half * 512:(half + 1) * 512], op=ALU.add),
                     reads=[pj_r[p], X_r], writes=[X_r])
            if fill["fn"] is not None:
                fill["fn"](4)
            yield
            S.dma(lambda e: e.dma_start(out=y_d[g * 128:(g + 1) * 128, :], in_=X[:, :]), reads=[X_r])

        NG = NSEQ * NT
        assert NT % 2 == 0
        catt = None
        d_queue = []
        it = 0
        next_pair = 0
        ROUNDS = 12

        def c_steps(n):
            nonlocal catt
            for _ in range(n):
                if catt is None:
                    return
                try:
                    next(catt[1])
                    catt[4] += 1
                except StopIteration:
                    d_queue.extend([2 * catt[0], 2 * catt[0] + 1])
                    catt = None

        while True:
            gens = []
            if 0 <= it - 1 < NG:
                gens.append(stage_B(it - 1))
            if it < NG:
                gens.append(stage_A(it))
            if d_queue:
                gens.append(stage_D(d_queue.pop(0)))
            if catt is None and next_pair * 2 + 1 <= it - 2 and next_pair * 2 < NG:
                i1 = (2 * next_pair) % NT + 1
                catt = [next_pair, stage_C(next_pair), it, 4 * (i1 + 1) + 1, 0]
                next_pair += 1
            if it - 1 > 0 and (it - 1) % NT == 0 and it - 1 < NG + 1:
                c_steps(10 ** 6)
            k = 1
            fill["fn"] = c_steps
            fill["rate"] = 0.0
            fill["credit"] = 0.0
            if catt is not None:
                remaining = catt[3] - catt[4]
                target = remaining if (it - catt[2] >= 1 or it >= NG) else (catt[3] + 1) // 2
                fill["rate"] = max(0.0, (target - ROUNDS)) / 130.0
                k = 1
            while gens:
                first = True
                for gen in list(gens):
                    try:
                        next(gen)
                    except StopIteration:
                        gens.remove(gen)
                    if first:
                        c_steps(k)
                        first = False
            if catt is not None and (it - catt[2] >= 1 or it >= NG):
                c_steps(10 ** 6)
            it += 1
            if it > NG + 1 and catt is None and not d_queue and next_pair * 2 >= NG:
                break

        sem_names = ["PE", "ACT", "DVE", "POOL"]
        sems = {n: es.enter_context(nc.semaphore(f"s_{n}")) for n in sem_names}
        dsems = [es.enter_context(nc.semaphore(f"d_{k}")) for k in range(NDS)]
        qsems = [es.enter_context(nc.semaphore(f"q_{k}")) for k in range(S.n_qdma)]
        block = es.enter_context(nc.Block())
        S.emit(nc, block, sems, dsems, qsems)
    return nc


def kernel(x, pre_norm_gain, w_in, fox_forget_bias, w_out, post_norm_gain):
    x = np.asarray(x, np.float32)
    B, SEQ, D = x.shape
    NSEQ = B // N_CORES
    NT = SEQ // 128
    consts, g128 = make_consts(SEQ)
    nc = build_nc(NSEQ, NT, g128)
    shared = dict(
        w_in=np.ascontiguousarray(np.asarray(w_in, np.float32)[0]),
        w_out=np.ascontiguousarray(np.asarray(w_out, np.float32)[0]),
        g1t=np.ascontiguousarray(np.asarray(pre_norm_gain, np.float32)[0].reshape(8, 128).T),
        g2=np.ascontiguousarray(np.asarray(post_norm_gain, np.float32)[0].reshape(1, D)),
        fb=np.ascontiguousarray(np.asarray(fox_forget_bias, np.float32)[0].reshape(1, FOX_HEADS)),
        **consts,
    )
    in_maps = []
    for c in range(N_CORES):
        m = dict(shared)
        m["x"] = np.ascontiguousarray(x[c * NSEQ:(c + 1) * NSEQ].reshape(NSEQ * SEQ, D))
        in_maps.append(m)
    res = run_bass_kernel_spmd(nc, in_maps, core_ids=list(range(N_CORES)))
    out = np.concatenate([np.asarray(r["y"], np.float32).reshape(NSEQ, SEQ, D) for r in res.results], axis=0)
    return out
```
